# Optimizing a Trainium2 kernel written in Bass

```python
import math
import jax, jax.numpy as jnp
from jax import lax
import numpy as np

D_MODEL = 2048
BATCH = 2
SEQ = 16384
DEPTH = 4

GRID_W = 64
CTX_LEN = 256
N_MIXERS = 2
N_ATTN_LAYERS = (DEPTH + 1) // 2
N_POOL_LAYERS = DEPTH // 2

DIFF_HEADS = 8
DIFF_HEAD_DIM = 128
DIFF_V_DIM = 2 * DIFF_HEAD_DIM
ATTN_WIDTH = DIFF_HEADS * DIFF_V_DIM
ROPE_THETA = 10000.0
ROPE_AXIS_PAIRS = DIFF_HEAD_DIM // 4
Q_BLOCK = 128

POOL_WINDOWS = (2, 4, 8, 16)
POOL_GROUPS = 4
POOL_WIDTH = D_MODEL
POOL_GROUP_DIM = POOL_WIDTH // POOL_GROUPS

DEEPNORM_ALPHA = (2 * DEPTH) ** 0.25
DEEPNORM_BETA = (8 * DEPTH) ** -0.25
LN_EPS = 1e-5
SUBLN_EPS = 1e-5

kernel_name = 'hybrid_diffattn_pool_deepnorm_dit'


def _layer_norm(x, g, b):
    xf = x.astype(jnp.float32)
    mu = jnp.mean(xf, axis=-1, keepdims=True)
    var = jnp.mean(jnp.square(xf - mu), axis=-1, keepdims=True)
    y = (xf - mu) * lax.rsqrt(var + LN_EPS)
    return (y * g.astype(jnp.float32) + b.astype(jnp.float32)).astype(x.dtype)


def _rotate(t, cos, sin):
    t1, t2 = jnp.split(t, 2, axis=-1)
    return jnp.concatenate([t1 * cos - t2 * sin, t2 * cos + t1 * sin], axis=-1)


def _rope_2d(t, cos_r, sin_r, cos_c, sin_c):
    tr, tc = jnp.split(t, 2, axis=-1)
    return jnp.concatenate([_rotate(tr, cos_r, sin_r), _rotate(tc, cos_c, sin_c)], axis=-1)


def _heads_qk(t):
    b, s, _ = t.shape
    return t.reshape(b, s, DIFF_HEADS, 2, DIFF_HEAD_DIM).transpose(0, 2, 3, 1, 4)


def _heads_v(t):
    b, s, _ = t.shape
    return t.reshape(b, s, DIFF_HEADS, DIFF_V_DIM).transpose(0, 2, 1, 3)


def _diff_attend(q, k, v, lam):
    b, h, _, sq, d = q.shape
    nb = sq // Q_BLOCK
    q = q * (1.0 / math.sqrt(d))
    qb = jnp.moveaxis(q.reshape(b, h, 2, nb, Q_BLOCK, d), 3, 0)

    def block(qi):
        s = jnp.einsum('bhcqd,bhckd->bhcqk', qi, k).astype(jnp.float32)
        p = jax.nn.softmax(s, axis=-1)
        a = (p[:, :, 0] - lam * p[:, :, 1]).astype(v.dtype)
        return jnp.einsum('bhqk,bhkv->bhqv', a, v)

    o = lax.map(block, qb)
    return jnp.moveaxis(o, 0, 2).reshape(b, h, sq, v.shape[-1])


def _diff_out(o, g, w_out, subln_g, lam_init):
    b, h, s, dv = o.shape
    of = o.astype(jnp.float32)
    of = of * lax.rsqrt(jnp.mean(jnp.square(of), axis=-1, keepdims=True) + SUBLN_EPS)
    of = of * subln_g.astype(jnp.float32) * (1.0 - lam_init)
    y = of.transpose(0, 2, 1, 3).reshape(b, s, h * dv).astype(g.dtype)
    return (y * jax.nn.silu(g)) @ w_out


def _attn_mixer(hx, hc, w_in, w_out, lq1, lk1, lq2, lk2, subln_g, lam_init, rope, ctx_out):
    dm = hx.shape[-1]
    lam = (jnp.exp(jnp.sum(lq1.astype(jnp.float32) * lk1.astype(jnp.float32)))
           - jnp.exp(jnp.sum(lq2.astype(jnp.float32) * lk2.astype(jnp.float32))) + lam_init)
    qx, kx, vx, gx = jnp.split(hx @ w_in, 4, axis=-1)
    qx = _rope_2d(_heads_qk(qx), *rope)
    kx = _rope_2d(_heads_qk(kx), *rope)
    vx = _heads_v(vx)
    if ctx_out:
        qc, kc, vc, gc = jnp.split(hc @ w_in, 4, axis=-1)
    else:
        kc, vc = jnp.split(hc @ w_in[:, dm:3 * dm], 2, axis=-1)
    kc = _heads_qk(kc)
    vc = _heads_v(vc)
    k_all = jnp.concatenate([kc, kx], axis=3)
    v_all = jnp.concatenate([vc, vx], axis=2)
    yx = _diff_out(_diff_attend(qx, k_all, v_all, lam), gx, w_out, subln_g, lam_init)
    if ctx_out:
        yc = _diff_out(_diff_attend(_heads_qk(qc), kc, vc, lam), gc, w_out, subln_g, lam_init)
    else:
        yc = None
    return yx, yc


def _centred_pool_minus_self(u, w):
    s = u.shape[1]
    lo_off = w // 2
    hi_off = w - 1 - lo_off
    uf = u.astype(jnp.float32)
    cs = jnp.cumsum(uf, axis=1)
    cs = jnp.pad(cs, ((0, 0), (1 + lo_off, 0), (0, 0)))
    cs = jnp.pad(cs, ((0, 0), (0, hi_off), (0, 0)), mode='edge')
    win = cs[:, w:w + s] - cs[:, :s]
    t = jnp.arange(s)
    cnt = (jnp.minimum(t + hi_off + 1, s) - jnp.maximum(t - lo_off, 0)).astype(jnp.float32)
    return (win / cnt[None, :, None] - uf).astype(u.dtype)


def _pool_mixer(h, w_in, grp_w, ch_scale, w_out):
    b, s, _ = h.shape
    u, g = jnp.split(h @ w_in, 2, axis=-1)
    groups = jnp.split(u, POOL_GROUPS, axis=-1)
    pooled = jnp.stack([_centred_pool_minus_self(ug, w) for ug, w in zip(groups, POOL_WINDOWS)], axis=-2)
    mixed = jnp.einsum('bsgc,gce->bsge', pooled, grp_w).reshape(b, s, POOL_WIDTH)
    return ((mixed * ch_scale) * jax.nn.silu(g)) @ w_out


def setup_inputs(seed: int = 0) -> dict:
    key = jax.random.key(seed)
    ks = jax.random.split(key, 20)
    D = D_MODEL
    f = jnp.float32
    nrm = lambda k, shape, s: jax.random.normal(k, shape, f) * s
    return {
        'x': nrm(ks[0], (BATCH, SEQ, D), 1.0),
        'c': nrm(ks[1], (BATCH, D), 1.0),
        'ctx': nrm(ks[2], (BATCH, CTX_LEN, D), 1.0),
        'c_ctx': nrm(ks[3], (D,), 1.0),
        'mod_w': nrm(ks[4], (DEPTH, D, 3 * D), 0.5 * D ** -0.5),
        'mod_b': nrm(ks[5], (DEPTH, 3 * D), 0.02),
        'ln_g': 1.0 + nrm(ks[6], (DEPTH, D), 0.02),
        'ln_b': nrm(ks[7], (DEPTH, D), 0.02),
        'attn_w_in': nrm(ks[8], (N_ATTN_LAYERS, D, 4 * ATTN_WIDTH), D ** -0.5),
        'attn_w_out': nrm(ks[9], (N_ATTN_LAYERS, ATTN_WIDTH, D), DEEPNORM_BETA * ATTN_WIDTH ** -0.5),
        'attn_lq1': nrm(ks[10], (N_ATTN_LAYERS, DIFF_HEAD_DIM), 0.1),
        'attn_lk1': nrm(ks[11], (N_ATTN_LAYERS, DIFF_HEAD_DIM), 0.1),
        'attn_lq2': nrm(ks[12], (N_ATTN_LAYERS, DIFF_HEAD_DIM), 0.1),
        'attn_lk2': nrm(ks[13], (N_ATTN_LAYERS, DIFF_HEAD_DIM), 0.1),
        'attn_subln_g': 1.0 + nrm(ks[14], (N_ATTN_LAYERS, DIFF_V_DIM), 0.02),
        'pool_w_in': nrm(ks[15], (N_POOL_LAYERS, D, 2 * POOL_WIDTH), D ** -0.5),
        'pool_grp_w': nrm(ks[16], (N_POOL_LAYERS, POOL_GROUPS, POOL_GROUP_DIM, POOL_GROUP_DIM), POOL_GROUP_DIM ** -0.5),
        'pool_scale': 1.0 + nrm(ks[17], (N_POOL_LAYERS, POOL_WIDTH), 0.02),
        'pool_w_out': nrm(ks[18], (N_POOL_LAYERS, POOL_WIDTH, D), DEEPNORM_BETA * POOL_WIDTH ** -0.5),
    }


def reference(x, c, ctx, c_ctx, mod_w, mod_b, ln_g, ln_b, attn_w_in, attn_w_out, attn_lq1, attn_lk1,
              attn_lq2, attn_lk2, attn_subln_g, pool_w_in, pool_grp_w, pool_scale, pool_w_out):
    s = x.shape[1]
    rows = s // GRID_W
    t_row = jnp.repeat(jnp.arange(rows, dtype=jnp.float32), GRID_W)
    t_col = jnp.tile(jnp.arange(GRID_W, dtype=jnp.float32), rows)
    inv_freq = ROPE_THETA ** (-jnp.arange(ROPE_AXIS_PAIRS, dtype=jnp.float32) / ROPE_AXIS_PAIRS)
    ang_r = t_row[:, None] * inv_freq
    ang_c = t_col[:, None] * inv_freq
    rope = (jnp.cos(ang_r).astype(x.dtype), jnp.sin(ang_r).astype(x.dtype),
            jnp.cos(ang_c).astype(x.dtype), jnp.sin(ang_c).astype(x.dtype))

    silu_c = jax.nn.silu(c)
    silu_cc = jax.nn.silu(c_ctx)

    for i in range(DEPTH):
        is_attn = (i % N_MIXERS) == 0
        ctx_out = any((j % N_MIXERS) == 0 for j in range(i + 1, DEPTH))
        shift, scale, gate = jnp.split(silu_c @ mod_w[i] + mod_b[i], 3, axis=-1)
        hx = x * (1.0 + scale[:, None, :]) + shift[:, None, :]
        if is_attn or ctx_out:
            shift_c, scale_c, gate_c = jnp.split(silu_cc @ mod_w[i] + mod_b[i], 3, axis=-1)
            hc = ctx * (1.0 + scale_c) + shift_c
        if is_attn:
            a = i // N_MIXERS
            lam_init = 0.8 - 0.6 * math.exp(-0.3 * i)
            yx, yc = _attn_mixer(hx, hc, attn_w_in[a], attn_w_out[a], attn_lq1[a], attn_lk1[a],
                                 attn_lq2[a], attn_lk2[a], attn_subln_g[a], lam_init, rope, ctx_out)
        else:
            p = i // N_MIXERS
            yx = _pool_mixer(hx, pool_w_in[p], pool_grp_w[p], pool_scale[p], pool_w_out[p])
            yc = _pool_mixer(hc, pool_w_in[p], pool_grp_w[p], pool_scale[p], pool_w_out[p]) if ctx_out else None
        x = _layer_norm(DEEPNORM_ALPHA * x + gate[:, None, :] * yx, ln_g[i], ln_b[i])
        if ctx_out:
            ctx = _layer_norm(DEEPNORM_ALPHA * ctx + gate_c * yc, ln_g[i], ln_b[i])
    return x
```

```python
import math
from contextlib import ExitStack
import numpy as np
import ml_dtypes
import concourse.bass as bass
import concourse.mybir as mybir
from concourse.bass_utils import run_bass_kernel_spmd

F32 = mybir.dt.float32
BF16 = mybir.dt.bfloat16
AF = mybir.ActivationFunctionType
ALU = mybir.AluOpType
AX = mybir.AxisListType

D = 2048
KC = 16
H = 8
CTX = 256
DEPTH = 4
ALPHA = (2 * DEPTH) ** 0.25
LN_EPS = 1e-5
SUBLN_EPS = 1e-5
POOL_WINDOWS = (2, 4, 8, 16)
NSD = 12
DBG = {"proj_stop": 99, "v_step": 9, "oq": "sp", "a1": 9}


class Op:
    __slots__ = ("eng", "fn", "deps", "sig", "sigval", "dma", "dsem", "dval", "prevdma", "xw", "cinc")

    def __init__(self, eng, fn, dma):
        self.eng = eng
        self.fn = fn
        self.dma = dma
        self.deps = ()
        self.sig = False
        self.sigval = 0
        self.dsem = None
        self.dval = 0
        self.prevdma = None
        self.xw = ()
        self.cinc = None


class Prog:
    ENG = ("pe", "act", "dve", "pool", "sp")
    QS = ("sp", "act", "pool")

    def __init__(self, nc, sync_same_engine=True):
        self.nc = nc
        self.sync_same = sync_same_engine
        self._cms = []
        self.sems = {}
        for e in self.ENG:
            cm = nc.semaphore("s_" + e)
            self.sems[e] = cm.__enter__()
            self._cms.append(cm)
        self.dsems = {}
        for q in self.QS:
            lst = []
            for i in range(NSD):
                cm = nc.semaphore("d_%s%d" % (q, i))
                lst.append(cm.__enter__())
                self._cms.append(cm)
            self.dsems[q] = lst
        self.sigcount = {e: 0 for e in self.ENG}
        self.dcount = {q: 0 for q in self.QS}
        self.dhist = {q: [] for q in self.QS}
        self.seen = {e: {} for e in self.ENG}
        self.nops = 0
        self._reset()

    def _reset(self):
        self.ops = {e: [] for e in self.ENG}
        self.last_w = {}
        self.readers = {}

    def close(self):
        for cm in reversed(self._cms):
            cm.__exit__(None, None, None)

    def _add(self, op, reads, writes):
        deps = set()
        for b in reads:
            w = self.last_w.get(b)
            if w is not None:
                deps.add(w)
        for b in writes:
            w = self.last_w.get(b)
            if w is not None:
                deps.add(w)
            rs = self.readers.get(b)
            if rs:
                deps.update(rs.values())
        deps.discard(op)
        for b in reads:
            d = self.readers.setdefault(b, {})
            key = (op.eng, id(op)) if op.dma else op.eng
            d[key] = op
        for b in writes:
            self.last_w[b] = op
            self.readers[b] = {}
        fd = []
        for d in deps:
            if (not d.dma) and (not op.dma) and d.eng == op.eng:
                if d.eng == "pe" or not self.sync_same:
                    continue
            if not d.dma:
                d.sig = True
            fd.append(d)
        op.deps = fd
        self.ops[op.eng].append(op)
        self.nops += 1
        return op

    def op(self, eng, fn, r=(), w=()):
        return self._add(Op(eng, fn, False), r, w)

    def dma(self, q, out, in_, r=(), w=(), xw=(), **kw):
        def fn(e):
            return e.dma_start(out=out, in_=in_, **kw)
        op = Op(q, fn, True)
        op.xw = xw
        n = self.dcount[q]
        self.dcount[q] = n + 1
        op.dsem = self.dsems[q][n % NSD]
        op.dval = 16 * (n // NSD + 1)
        h = self.dhist[q]
        if len(h) >= NSD:
            op.prevdma = h[-NSD]
        h.append(op)
        if len(h) > 2 * NSD:
            del h[0:len(h) - 2 * NSD]
        return self._add(op, r, w)

    def flush(self):
        nc = self.nc
        fin = Op("sp", None, False)
        deps = []
        for q in self.QS:
            deps.extend(self.dhist[q][-NSD:])
        for e in self.ENG:
            if e != "sp" and self.ops[e]:
                last = self.ops[e][-1]
                if not last.dma:
                    last.sig = True
                deps.append(last)
        fin.deps = deps
        self.ops["sp"].append(fin)
        for e in self.ENG:
            c = self.sigcount[e]
            for o in self.ops[e]:
                if o.sig and not o.dma:
                    c += 1
                    o.sigval = c
            self.sigcount[e] = c
        handles = {"pe": "tensor", "act": "scalar", "dve": "vector", "pool": "gpsimd", "sp": "sync"}
        with nc.Block() as block:
            for e in self.ENG:
                ops = self.ops[e]
                if not ops:
                    continue

                def body(eng, ops=ops, seen=self.seen[e], sem_e=self.sems[e]):
                    for o in ops:
                        if o.dma and o.prevdma is not None:
                            p = o.prevdma
                            k = id(p.dsem)
                            if seen.get(k, 0) < p.dval:
                                eng.wait_ge(p.dsem, p.dval)
                                seen[k] = p.dval
                        for d in o.deps:
                            if d.dma:
                                s, v = d.dsem, d.dval
                            else:
                                s, v = self.sems[d.eng], d.sigval
                            k = id(s)
                            if seen.get(k, 0) < v:
                                eng.wait_ge(s, v)
                                seen[k] = v
                        for (xs_, xv_) in o.xw:
                            k = id(xs_)
                            if seen.get(k, 0) < xv_:
                                eng.wait_ge(xs_, xv_)
                                seen[k] = xv_
                        if o.fn is None:
                            eng.nop()
                            continue
                        inst = o.fn(eng)
                        if o.cinc is not None:
                            inst.then_inc(o.cinc[0], o.cinc[1])
                        elif o.dma:
                            inst.then_inc(o.dsem, 16)
                        elif o.sig:
                            inst.then_inc(sem_e, 1)

                getattr(block, handles[e])(body)
        self._reset()


class Alloc:
    _uid = [0]

    def __init__(self, nc):
        self.nc = nc
        self.es = ExitStack()
        Alloc._uid[0] += 1
        self.sfx = "_%d" % Alloc._uid[0]

    def sb(self, name, shape, dt):
        return self.es.enter_context(self.nc.sbuf_tensor(name + self.sfx, shape, dt))

    def ps(self, name, shape, dt):
        return self.es.enter_context(self.nc.psum_tensor(name + self.sfx, shape, dt))

    def __enter__(self):
        return self

    def __exit__(self, *a):
        self.es.close()
        return False


def mm(P, out, lhsT, rhs, start, stop, r, w):
    P.op("pe", lambda e: e.matmul(out, lhsT=lhsT, rhs=rhs, start=start, stop=stop), r, w)


def tr(P, out, in_, ident, r, w):
    P.op("pe", lambda e: e.transpose(out=out, in_=in_, identity=ident), r, w)


def act(P, out, in_, func, r, w, scale=None, bias=None):
    kw = {}
    if scale is not None:
        kw["scale"] = scale
    if bias is not None:
        kw["bias"] = bias
    P.op("act", lambda e: e.activation(out=out, in_=in_, func=func, **kw), r, w)


def tt(P, eng, out, in0, in1, op, r, w):
    P.op(eng, lambda e: e.tensor_tensor(out=out, in0=in0, in1=in1, op=op), r, w)


def ts(P, eng, out, in0, s1, s2, op0, op1, r, w):
    if op1 is None:
        P.op(eng, lambda e: e.tensor_scalar(out=out, in0=in0, scalar1=s1, scalar2=None, op0=op0), r, w)
    else:
        P.op(eng, lambda e: e.tensor_scalar(out=out, in0=in0, scalar1=s1, scalar2=s2, op0=op0, op1=op1), r, w)


def stt(P, out, in0, scalar, in1, op0, op1, r, w):
    P.op("dve", lambda e: e.scalar_tensor_tensor(out=out, in0=in0, scalar=scalar, in1=in1, op0=op0, op1=op1), r, w)


def cp(P, eng, out, in_, r, w):
    P.op(eng, lambda e: e.tensor_copy(out=out, in_=in_), r, w)


def recip(P, out, in_, r, w):
    P.op("dve", lambda e: e.reciprocal(out=out, in_=in_), r, w)


class Cfg:
    def __init__(self, S):
        self.S = S
        self.NOWN = S // 4
        self.NTO = self.NOWN // 128
        self.SCT = min(16, self.NTO)
        self.NK = CTX + S
        self.NKT = self.NK // 128
        self.QW = min(512, self.NOWN)
        self.VCH = min(256, self.NOWN)
        self.NVC = self.NOWN // self.VCH


class Build:
    def __init__(self, cfg, ext_in, ext_out):
        self.cfg = cfg
        self.nc = bass.Bass("TRN2", target_bir_lowering=False)
        self.ext_in = set(ext_in)
        self.ext_out = set(ext_out)
        self.P = Prog(self.nc)
        self.tensors = {}
        self.used_in = []
        self.used_out = []

    def dram(self, name, shape, dt):
        if name in self.tensors:
            return self.tensors[name]
        if name in self.ext_in:
            kind = "ExternalInput"
            self.used_in.append(name)
        elif name in self.ext_out:
            kind = "ExternalOutput"
            self.used_out.append(name)
        else:
            kind = "Internal"
        if False and kind != "Internal" and dt == BF16:
            t = self.nc.dram_tensor(name, list(shape), mybir.dt.uint16, kind=kind).ap().bitcast(BF16)
        else:
            t = self.nc.dram_tensor(name, list(shape), dt, kind=kind).ap()
        self.tensors[name] = t
        return t

    def t_x(self, i):
        c = self.cfg
        return self.dram("x%d" % i, [c.NOWN, D], F32)

    def t_c(self, i):
        return self.dram("c%d" % i, [CTX, D], F32)

    def t_modv(self):
        return self.dram("modv", [DEPTH, 2, 3 * D], F32)

    def t_qt(self):
        return self.dram("QT", [16, 128, self.cfg.NOWN], BF16)

    def t_gt(self):
        return self.dram("GT", [16, 128, self.cfg.NOWN], BF16)

    def t_qtc(self):
        return self.dram("QTc", [16, 128, CTX], BF16)

    def t_gtc(self):
        return self.dram("GTc", [16, 128, CTX], BF16)

    def t_ktc(self):
        return self.dram("KTc", [16, 128, CTX], BF16)

    def t_vc(self):
        return self.dram("Vc", [CTX, D], BF16)

    def t_kt_own(self):
        return self.dram("KT_own", [16, 128, self.cfg.NOWN], BF16)

    def t_v_own(self):
        return self.dram("V_own", [H, self.cfg.NOWN, 256], BF16)

    def t_kt_all(self):
        return self.dram("KT_all", [16, 4, 128, self.cfg.NOWN], BF16)

    def t_v_all(self):
        c = self.cfg
        return self.dram("V_all", [H, 2, 4, c.NOWN // 2, 256], BF16)

    def t_u(self):
        return self.dram("U", [self.cfg.NOWN + 128 + CTX, D], BF16)

    def t_zt(self):
        return self.dram("ZT", [16, 128, self.cfg.NOWN], BF16)

    def t_ztc(self):
        return self.dram("ZTc", [16, 128, CTX], BF16)

    def t_xhalo(self):
        return self.dram("xhalo", [128, D], F32)


def load_featmajor_vec(P, q, out_sb, dram_vec, w):
    P.dma(q, out_sb, dram_vec.rearrange("(k p) -> p k", p=128), w=w, allow_slow_non_contiguous=True)


def stage_mod(B):
    nc, P = B.nc, B.P
    MC = 3 * D // 4
    cvec = B.dram("cvec", [2, D], F32)
    mod_w = B.dram("mod_w", [DEPTH, D, MC], F32)
    mod_b = B.dram("mod_b", [DEPTH, MC], F32)
    modv = B.t_modv()
    modp = B.dram("modp", [DEPTH * 2, MC], F32)
    modg = B.dram("modg", [4 * DEPTH * 2, MC], F32)
    A = Alloc(nc)
    cv = A.sb("m_cv", [128, KC, 2], F32)
    cvl = A.sb("m_cvl", [128, 2, KC], F32)
    w0 = A.sb("m_w0", [128, KC, 512], F32)
    w1 = A.sb("m_w1", [128, KC, 512], F32)
    bb = A.sb("m_b", [2, MC], F32)
    row = A.sb("m_row", [2, MC], F32)
    ps0 = A.ps("m_ps0", [128, 512], F32)
    ps1 = A.ps("m_ps1", [128, 512], F32)
    with A:
        wb = [w0, w1]
        pss = [ps0, ps1]
        for who in range(2):
            P.dma("sp", cvl[:, who, :], cvec[who].rearrange("(k p) -> p k", p=128), w=[("cvl", who)], allow_slow_non_contiguous=True)
            act(P, cvl[:, who, :], cvl[:, who, :], AF.Silu, [("cvl", who)], [("cvl", who)])
            cp(P, "dve", cv[:, :, who], cvl[:, who, :], [("cvl", who)], ["cv"])
        n = 0
        for i in range(DEPTH):
            P.dma("sp", bb[:], mod_b[i:i + 1, :].partition_broadcast(2), r=(), w=["bb"])
            for nb in range(MC // 512):
                s = n % 2
                q = "sp" if n % 2 == 0 else "pool"
                P.dma(q, wb[s][:], mod_w[i, :, nb * 512:(nb + 1) * 512].rearrange("(k p) n -> p k n", p=128), w=[("w", s)])
                for k in range(KC):
                    mm(P, pss[s][0:2, :], cv[:, k, :], wb[s][:, k, :], k == 0, k == KC - 1, ["cv", ("w", s)], [("ps", s)])
                tt(P, "dve", row[:, nb * 512:(nb + 1) * 512], pss[s][0:2, :], bb[:, nb * 512:(nb + 1) * 512], ALU.add,
                   [("ps", s), "bb"], ["row"])
                n += 1
            P.dma("sp", modp[i * 2:(i + 1) * 2, :], row[:], r=["row"], w=())
        collective_allgather(B, [(modp, modg)])
        for r in range(4):
            P.dma("sp", modv[:, :, r * MC:(r + 1) * MC], modg[r * 8:(r + 1) * 8, :].rearrange("(l w) j -> l w j", w=2))
        P.flush()


def stage_proj(B, layer):
    nc, P, cfg = B.nc, B.P, B.cfg
    is_attn = layer % 2 == 0
    a = layer // 2
    ctx_needed = layer <= 2
    ctx_full = layer in (0, 1)
    if is_attn:
        w_in = B.dram("attn_w_in", [2, D, 4 * D], F32)[a]
        ncols = 4 * D
    else:
        w_in = B.dram("pool_w_in", [2, D, 2 * D], F32)[a]
        ncols = 2 * D
    modv = B.t_modv()
    ident = B.dram("ident", [128, 128], F32)
    rope_c = B.dram("rope_c", [128, cfg.NOWN], F32)
    rope_s = B.dram("rope_s", [128, cfg.NOWN], F32)
    x_src = B.t_x(layer)
    c_src = B.t_c(layer) if ctx_needed else None
    NOWN, NTO, SCT = cfg.NOWN, cfg.NTO, cfg.SCT

    tiles = [("own", t) for t in range(NTO)]
    extra = []
    if not is_attn:
        extra.append(("halo", 0))
    if ctx_needed:
        extra += [("ctx", 0), ("ctx", 1)]
    scs = [tiles[i:i + SCT] for i in range(0, NTO, SCT)]
    scs[-1] = scs[-1] + extra

    def src_ap(kind, t):
        if kind == "own":
            return x_src[t * 128:(t + 1) * 128, :]
        if kind == "halo":
            return B.t_xhalo()[:, :]
        return c_src[t * 128:(t + 1) * 128, :]

    if is_attn:
        blk_kind = ["q"] * 4 + ["k"] * 4 + ["v"] * 4 + ["g"] * 4
    else:
        blk_kind = ["u"] * 4 + ["g"] * 4
    nblk = ncols // 512

    A = Alloc(nc)
    hxT = A.sb("p_hxT", [128, KC, (SCT + 3) * 128], BF16)
    xs0 = A.sb("p_x0", [128, D], F32)
    xs1 = A.sb("p_x1", [128, D], F32)
    wb0 = A.sb("p_w0", [128, KC, 512], BF16)
    wb1 = A.sb("p_w1", [128, KC, 512], BF16)
    rc = A.sb("p_rc", [128, max(SCT, 3) * 128], F32)
    rs = A.sb("p_rs", [128, max(SCT, 3) * 128], F32)
    idt = A.sb("p_id", [128, 128], F32)
    modt = A.sb("p_mod", [128, 2, 2, KC], F32)
    qf = A.sb("p_qf", [128, 2, 512], F32)
    ta = A.sb("p_ta", [128, 2, 512], F32)
    tb = A.sb("p_tb", [128, 2, 512], F32)
    ob = A.sb("p_ob", [128, 4, 512], BF16)
    pt0 = A.ps("p_pt0", [128, 512], F32)
    pt1 = A.ps("p_pt1", [128, 512], F32)
    pa0 = A.ps("p_pa0", [128, 512], F32)
    pa1 = A.ps("p_pa1", [128, 512], F32)
    pa2 = A.ps("p_pa2", [128, 512], F32)
    pa3 = A.ps("p_pa3", [128, 512], F32)
    with A:
        xs = [xs0, xs1]
        wbs = [wb0, wb1]
        ptr = [pt0, pt1]
        pacc = [pa0, pa1, pa2, pa3]
        P.dma("sp", idt[:], ident, w=["id"])
        for who in range(2):
            load_featmajor_vec(P, "sp", modt[:, who, 0, :], modv[layer, who, 0:D], [("mod", who, 0)])
            load_featmajor_vec(P, "sp", modt[:, who, 1, :], modv[layer, who, D:2 * D], [("mod", who, 1)])
            ts(P, "dve", modt[:, who, 1, :], modt[:, who, 1, :], 1.0, None, ALU.add, None, [("mod", who, 1)], [("mod", who, 1)])
        P.flush()

        cnt = {"x": 0, "pt": 0, "w": 0, "pa": 0, "ob": 0, "st": 0}
        if DBG["proj_stop"] <= 0:
            scs = []
        for sc in scs:
            ntile = len(sc)
            ntok = ntile * 128
            n_own = sum(1 for kd, _ in sc if kd == "own")
            if is_attn:
                t0 = sc[0][1]
                P.dma("sp", rc[:, 0:n_own * 128], rope_c[:, t0 * 128:(t0 + n_own) * 128], w=["rc"])
                P.dma("sp", rs[:, 0:n_own * 128], rope_s[:, t0 * 128:(t0 + n_own) * 128], w=["rs"])
            nchunks = [(i0 * 128, min(4, n_own - i0) * 128, True) for i0 in range(0, n_own, 4)]
            if ntile > n_own:
                nchunks.append((n_own * 128, (ntile - n_own) * 128, False))
            for ti, (kind, t) in enumerate(sc):
                who = 1 if kind == "ctx" else 0
                s = cnt["x"] % 2
                cnt["x"] += 1
                P.dma("sp" if s == 0 else "pool", xs[s][:], src_ap(kind, t), w=[("x", s)])
                for g4 in range(4):
                    ps = cnt["pt"] % 2
                    cnt["pt"] += 1
                    if DBG["a1"] < 1:
                        continue
                    for j in range(4):
                        c = g4 * 4 + j
                        tr(P, ptr[ps][:, j * 128:(j + 1) * 128], xs[s][:, c * 128:(c + 1) * 128], idt[:],
                           [("x", s)], [("pt", ps)])
                    if DBG["a1"] < 2:
                        continue
                    for j in range(4):
                        c = g4 * 4 + j
                        ts(P, "dve", hxT[:, c, ti * 128:(ti + 1) * 128], ptr[ps][:, j * 128:(j + 1) * 128],
                           modt[:, who, 1, c:c + 1], modt[:, who, 0, c:c + 1], ALU.mult, ALU.add, [("pt", ps)], [("hx", ti)])
            for cb in range(nblk):
                kind_b = blk_kind[cb]
                if DBG["proj_stop"] <= 1:
                    continue
                if DBG["proj_stop"] == 2 and kind_b not in ("v", "u"):
                    continue
                if DBG["proj_stop"] == 3 and kind_b not in ("v", "u", "g"):
                    continue
                ws = cnt["w"] % 2
                cnt["w"] += 1
                P.dma("pool", wbs[ws][:], w_in[:, cb * 512:(cb + 1) * 512].rearrange("(k p) n -> p k n", p=128), w=[("w", ws)])
                hx_keys = [("hx", ti) for ti in range(ntile)]
                if kind_b in ("v", "u"):
                    for ti, (kind, t) in enumerate(sc):
                        if kind_b == "v" and kind == "halo":
                            continue
                        pa = cnt["pa"] % 4
                        cnt["pa"] += 1
                        if DBG["v_step"] < 1:
                            continue
                        for k in range(KC):
                            mm(P, pacc[pa][:], hxT[:, k, ti * 128:(ti + 1) * 128], wbs[ws][:, k, :], k == 0, k == KC - 1,
                               [("hx", ti), ("w", ws)], [("pa", pa)])
                        if DBG["v_step"] < 2:
                            continue
                        o = cnt["ob"] % 4
                        cnt["ob"] += 1
                        if cnt["st"] % 2 == 0:
                            act(P, ob[:, o, :], pacc[pa][:], AF.Copy, [("pa", pa)], [("ob", o)])
                        else:
                            cp(P, "dve", ob[:, o, :], pacc[pa][:], [("pa", pa)], [("ob", o)])
                        cnt["st"] += 1
                        if DBG["v_step"] < 3:
                            continue
                        cols = slice((cb % 4) * 512, (cb % 4 + 1) * 512)
                        if kind_b == "v":
                            if kind == "own":
                                for hh in range(2):
                                    P.dma(DBG["oq"], B.t_v_own()[(cb % 4) * 2 + hh, t * 128:(t + 1) * 128, :],
                                          ob[:, o, hh * 256:(hh + 1) * 256], r=[("ob", o)])
                                continue
                            dst = B.t_vc()[t * 128:(t + 1) * 128, cols]
                        else:
                            if kind == "own":
                                row0 = t * 128
                            elif kind == "halo":
                                row0 = NOWN
                            else:
                                row0 = NOWN + 128 + t * 128
                            dst = B.t_u()[row0:row0 + 128, cols]
                        P.dma(DBG["oq"], dst, ob[:, o, :], r=[("ob", o)])
                    continue
                for j in range(4):
                    cc = (cb % 4) * 4 + j
                    for (n0, nw, own_sc) in nchunks:
                        if not own_sc:
                            if not any(kd == "ctx" for kd, _ in sc):
                                continue
                            if kind_b in ("q",) and not ctx_full:
                                continue
                            if kind_b == "g" and (not ctx_full) and is_attn:
                                continue
                        pa = cnt["pa"] % 4
                        cnt["pa"] += 1
                        t_lo, t_hi = n0 // 128, (n0 + nw) // 128
                        for k in range(KC):
                            mm(P, pacc[pa][:, 0:nw], wbs[ws][:, k, j * 128:(j + 1) * 128], hxT[:, k, n0:n0 + nw],
                               k == 0, k == KC - 1, hx_keys[t_lo:t_hi] + [("w", ws)], [("pa", pa)])
                        o = cnt["ob"] % 4
                        cnt["ob"] += 1
                        if kind_b == "g":
                            act(P, ob[:, o, 0:nw], pacc[pa][:, 0:nw], AF.Silu, [("pa", pa)], [("ob", o)])
                        elif own_sc and kind_b in ("q", "k"):
                            f = cnt["st"] % 2
                            cnt["st"] += 1
                            act(P, qf[:, f, 0:nw], pacc[pa][:, 0:nw], AF.Copy, [("pa", pa)], [("qf", f)])
                            tt(P, "dve", ta[:, f, 0:nw], qf[:, f, 0:nw], rc[:, n0:n0 + nw], ALU.mult, [("qf", f), "rc"], [("ta", f)])
                            for qd, (o0, i0) in enumerate(((0, 32), (32, 0), (64, 96), (96, 64))):
                                tt(P, "pool", tb[o0:o0 + 32, f, 0:nw], qf[i0:i0 + 32, f, 0:nw], rs[i0:i0 + 32, n0:n0 + nw], ALU.mult,
                                   [("qf", f), "rs"], [("tb", f, qd)])
                            tt(P, "dve", ob[:, o, 0:nw], ta[:, f, 0:nw], tb[:, f, 0:nw], ALU.add,
                               [("ta", f)] + [("tb", f, qd) for qd in range(4)], [("ob", o)])
                        else:
                            act(P, ob[:, o, 0:nw], pacc[pa][:, 0:nw], AF.Copy, [("pa", pa)], [("ob", o)])
                        if own_sc:
                            tok0 = sc[0][1] * 128 + n0
                            dstT = {"q": B.t_qt, "k": B.t_kt_own, "g": B.t_gt}[kind_b]()
                            P.dma("sp", dstT[cc, :, tok0:tok0 + nw], ob[:, o, 0:nw], r=[("ob", o)])
                        else:
                            for ti, (kind, t) in enumerate(sc):
                                if kind != "ctx":
                                    continue
                                dstT = {"q": B.t_qtc, "k": B.t_ktc, "g": B.t_gtc}[kind_b]()
                                P.dma("sp", dstT[cc, :, t * 128:(t + 1) * 128], ob[:, o, ti * 128 - n0:(ti + 1) * 128 - n0], r=[("ob", o)])
            P.flush()


def stage_attn(B, layer):
    nc, P, cfg = B.nc, B.P, B.cfg
    a = layer // 2
    lam_init = 0.8 - 0.6 * math.exp(-0.3 * layer)
    ctx_q = layer == 0
    NOWN, NTO, NKT, QW = cfg.NOWN, cfg.NTO, cfg.NKT, cfg.QW
    NK = cfg.NK
    qt, gt, ktc, vc = B.t_qt(), B.t_gt(), B.t_ktc(), B.t_vc()
    kt_all, v_all = B.t_kt_all(), B.t_v_all()
    zt = B.t_zt()
    lqk = [B.dram(n, [2, 128], F32) for n in ("attn_lq1", "attn_lk1", "attn_lq2", "attn_lk2")]
    subln = B.dram("attn_subln_g", [2, 256], F32)
    scale = 1.0 / math.sqrt(128.0)

    A = Alloc(nc)
    Kt = A.sb("a_kt", [128, 2, NK], BF16)
    Vt = A.sb("a_vt", [128, NKT, 256], BF16)
    Qs = A.sb("a_q", [128, 2, 2, 512], BF16)
    PT = A.sb("a_pt", [128, 4, 2, 512], BF16)
    Pacc = A.sb("a_pacc", [128, 2, 512], F32)
    Paccb = A.sb("a_paccb", [128, 2, 2, 512], BF16)
    ones_b = A.sb("a_ones", [128, 128], BF16)
    ones_f = A.sb("a_onesf", [128, 128], F32)
    lq = A.sb("a_lq", [128, 4, 128], F32)
    sc = A.sb("a_sc", [128, 8], F32)
    sg = A.sb("a_sg", [128, 2], F32)
    Oc = A.sb("a_oc", [128, 4, 512], F32)
    Lc = A.sb("a_lc", [128, 2, 512], F32)
    AB = A.sb("a_ab", [128, 2, 512], F32)
    Ys = AB
    Oo = A.sb("a_o", [128, 2, 512], F32)
    Sq = A.sb("a_sq", [128, 2, 512], F32)
    Rt = A.sb("a_rt", [128, 512], F32)
    Gs = A.sb("a_g", [128, 2, 512], BF16)
    Zs = A.sb("a_z", [128, 2, 512], BF16)
    s0 = A.ps("a_s0", [128, 512], F32)
    s1 = A.ps("a_s1", [128, 512], F32)
    o00 = A.ps("a_o00", [128, 512], F32)
    o01 = A.ps("a_o01", [128, 512], F32)
    o10 = A.ps("a_o10", [128, 512], F32)
    o11 = A.ps("a_o11", [128, 512], F32)
    l0 = A.ps("a_l0", [128, 512], F32)
    l1 = A.ps("a_l1", [128, 512], F32)
    with A:
        Sps = [s0, s1]
        Ops = [[o00, o01], [o10, o11]]
        Lps = [l0, l1]
        P.op("dve", lambda e: e.memset(ones_b[:], 1.0), (), ["c_ones"])
        P.op("dve", lambda e: e.memset(ones_f[:], 1.0), (), ["c_onesf"])
        P.op("dve", lambda e: e.memset(sc[:, 6:7], SUBLN_EPS), (), ["c_eps"])
        for i in range(4):
            P.dma("sp", lq[:, i, :], lqk[i][a:a + 1, :].partition_broadcast(128), w=[("lq", i)])
        tt(P, "dve", lq[:, 0, :], lq[:, 0, :], lq[:, 1, :], ALU.mult, [("lq", 0), ("lq", 1)], [("lq", 0)])
        tt(P, "dve", lq[:, 2, :], lq[:, 2, :], lq[:, 3, :], ALU.mult, [("lq", 2), ("lq", 3)], [("lq", 2)])
        P.op("dve", lambda e: e.tensor_reduce(out=sc[:, 0:1], in_=lq[:, 0, :], axis=AX.X, op=ALU.add), [("lq", 0)], ["sc0"])
        P.op("dve", lambda e: e.tensor_reduce(out=sc[:, 1:2], in_=lq[:, 2, :], axis=AX.X, op=ALU.add), [("lq", 2)], ["sc1"])
        act(P, sc[:, 2:3], sc[:, 0:1], AF.Exp, ["sc0"], ["sc2"])
        act(P, sc[:, 3:4], sc[:, 1:2], AF.Exp, ["sc1"], ["sc3"])
        tt(P, "dve", sc[:, 4:5], sc[:, 3:4], sc[:, 2:3], ALU.subtract, ["sc2", "sc3"], ["sc4"])
        ts(P, "dve", sc[:, 5:6], sc[:, 4:5], -lam_init, None, ALU.add, None, ["sc4"], ["neglam"])
        P.dma("sp", sg[:], subln[a].rearrange("(c p) -> p c", p=128), w=["sg"], allow_slow_non_contiguous=True)
        ts(P, "dve", sg[:], sg[:], 1.0 - lam_init, None, ALU.mult, None, ["sg"], ["sg"])
        P.flush()
        neglam = sc[:, 5:6]
        eps_ap = sc[:, 6:7]

        pend = []

        def epi2(h, tok0, nq, is_ctx):
            mm(P, Sps[0][:, 0:nq], ones_f[:], Sq[:, 0, 0:nq], True, False, [("sq", 0)], [("S", 0)])
            mm(P, Sps[0][:, 0:nq], ones_f[:], Sq[:, 1, 0:nq], False, True, [("sq", 1)], [("S", 0)])
            act(P, Rt[:, 0:nq], Sps[0][:, 0:nq], AF.Ln, [("S", 0)], ["rt"], scale=1.0 / 256.0, bias=eps_ap)
            act(P, Rt[:, 0:nq], Rt[:, 0:nq], AF.Exp, ["rt"], ["rt"], scale=-0.5)
            gsrc = B.t_gtc() if is_ctx else gt
            zdst = B.t_ztc() if is_ctx else zt
            for dv in range(2):
                P.dma("pool", Gs[:, dv, 0:nq], gsrc[h * 2 + dv, :, tok0:tok0 + nq], w=[("g", dv)])
                stt(P, Ys[:, dv, 0:nq], Oo[:, dv, 0:nq], sg[:, dv:dv + 1], Rt[:, 0:nq], ALU.mult, ALU.mult,
                    [("o", dv), "rt"], [("ab", dv)])
                tt(P, "pool", Zs[:, dv, 0:nq], Ys[:, dv, 0:nq], Gs[:, dv, 0:nq], ALU.mult, [("ab", dv), ("g", dv)], [("z", dv)])
                P.dma("sp", zdst[h * 2 + dv, :, tok0:tok0 + nq], Zs[:, dv, 0:nq], r=[("z", dv)])

        kt_own, v_own = B.t_kt_own(), B.t_v_own()
        HN = NOWN // 2
        hsems = []
        for h in range(H):
            _ccuid[0] += 1
            cm = nc.semaphore("cc%d" % _ccuid[0])
            hsems.append(cm.__enter__())
            P._cms.append(cm)
        for h in range(H):
            prs = [(kt_own[h * 2 + c], kt_all[h * 2 + c].rearrange("r p n -> (r p) n")) for c in range(2)]
            prs += [(v_own[h, hf * HN:(hf + 1) * HN, :], v_all[h, hf].rearrange("r t n -> (r t) n")) for hf in range(2)]
            for (ci, co) in prs:
                o_ = P.op("pool", (lambda e, ci=ci, co=co: e.collective_compute(
                    "AllGather", ALU.bypass, replica_groups=RG, ins=[ci.opt()], outs=[co.opt()])), (), ())
                o_.cinc = (hsems[h], 1)
        qn = 0
        for h in range(H):
            hw = [(hsems[h], 4)]
            for c in range(2):
                hc = h * 2 + c
                P.dma("sp", Kt[:, c, 0:CTX], ktc[hc], w=[("K", c, 0), ("K", c, 1)])
                for r in range(4):
                    for t8 in range(0, NTO, 8):
                        n8 = min(8, NTO - t8)
                        k0 = CTX + r * NOWN + t8 * 128
                        P.dma("sp", Kt[:, c, k0:k0 + n8 * 128],
                              kt_all[hc, r, :, t8 * 128:(t8 + n8) * 128],
                              w=[("K", c, 2 + r * NTO + t8 + i) for i in range(n8)], xw=hw)
            P.dma("sp", Vt[:, 0:2, :], vc[:, h * 256:(h + 1) * 256].rearrange("(t p) n -> p t n", p=128),
                  w=[("V", 0), ("V", 1)])
            tph = HN // 128
            for r in range(4):
                for hf in range(2):
                    for t8 in range(0, tph, 8):
                        n8 = min(8, tph - t8)
                        kt0 = 2 + r * NTO + hf * tph + t8
                        P.dma("sp", Vt[:, kt0:kt0 + n8, :],
                              v_all[h, hf, r, t8 * 128:(t8 + n8) * 128, :].rearrange("(t p) n -> p t n", p=128),
                              w=[("V", kt0 + i) for i in range(n8)], xw=hw)
            chunks = [(False, q0, min(QW, NOWN - q0)) for q0 in range(0, NOWN, QW)]
            if ctx_q:
                chunks.append((True, 0, CTX))
            for (is_ctx, tok0, nq) in chunks:
                qs = qn % 2
                qn += 1
                qsrc = B.t_qtc() if is_ctx else qt
                for c in range(2):
                    P.dma("sp", Qs[:, qs, c, 0:nq], qsrc[h * 2 + c, :, tok0:tok0 + nq], w=[("Q", qs, c)])
                kts = list(range(2)) if is_ctx else list(range(NKT))
                nkt = len(kts)

                def emit_S(i):
                    kt = kts[i]
                    for c in range(2):
                        mm(P, Sps[c][:, 0:nq], Kt[:, c, kt * 128:(kt + 1) * 128], Qs[:, qs, c, 0:nq], True, True,
                           [("K", c, kt), ("Q", qs, c)], [("S", c)])
                        act(P, PT[:, i % 4, c, 0:nq], Sps[c][:, 0:nq], AF.Exp, [("S", c)], [("PT", i % 4, c)], scale=scale)

                def emit_AV(i):
                    kt = kts[i]
                    st, sp_ = (i == 0), (i == nkt - 1)
                    for dv in range(2):
                        for c in range(2):
                            mm(P, Ops[c][dv][:, 0:nq], Vt[:, kt, dv * 128:(dv + 1) * 128], PT[:, i % 4, c, 0:nq], st, sp_,
                               [("V", kt), ("PT", i % 4, c)], [("O", c, dv)])
                    g0 = (i // GL) * GL
                    gsz = min(GL, nkt - g0)
                    gi = i - g0
                    last = gi == gsz - 1
                    gpar = (i // GL) % 2
                    for c in range(2):
                        eng = "dve" if c == 0 else "pool"
                        if gsz == 1:
                            mm(P, Lps[c][:, 0:nq], ones_b[:], PT[:, i % 4, c, 0:nq], g0 == 0, sp_, [("PT", i % 4, c)], [("L", c)])
                            continue
                        if gi == 0:
                            continue
                        dst, dkey = (Paccb[:, gpar, c, 0:nq], ("paccb", gpar, c)) if last else (Pacc[:, c, 0:nq], ("pacc", c))
                        if gi == 1:
                            tt(P, eng, dst, PT[:, (i - 1) % 4, c, 0:nq], PT[:, i % 4, c, 0:nq], ALU.add,
                               [("PT", (i - 1) % 4, c), ("PT", i % 4, c)], [dkey])
                        else:
                            tt(P, eng, dst, Pacc[:, c, 0:nq], PT[:, i % 4, c, 0:nq], ALU.add,
                               [("pacc", c), ("PT", i % 4, c)], [dkey])
                        if last:
                            mm(P, Lps[c][:, 0:nq], ones_b[:], Paccb[:, gpar, c, 0:nq], g0 == 0, sp_, [("paccb", gpar, c)], [("L", c)])

                emit_S(0)
                for i in range(nkt):
                    if i + 1 < nkt:
                        emit_S(i + 1)
                    emit_AV(i)
                    if i == min(16, nkt - 1) and pend:
                        epi2(*pend.pop())
                if pend:
                    epi2(*pend.pop())
                for c in range(2):
                    cp(P, "dve", Lc[:, c, 0:nq], Lps[c][:, 0:nq], [("L", c)], [("lc", c)])
                    for dv in range(2):
                        if (c + dv) % 2 == 0:
                            cp(P, "dve", Oc[:, c * 2 + dv, 0:nq], Ops[c][dv][:, 0:nq], [("O", c, dv)], [("oc", c, dv)])
                        else:
                            act(P, Oc[:, c * 2 + dv, 0:nq], Ops[c][dv][:, 0:nq], AF.Copy, [("O", c, dv)], [("oc", c, dv)])
                for c in range(2):
                    recip(P, Lc[:, c, 0:nq], Lc[:, c, 0:nq], [("lc", c)], [("lc", c)])
                for dv in range(2):
                    tt(P, "pool", AB[:, 0, 0:nq], Oc[:, dv, 0:nq], Lc[:, 0, 0:nq], ALU.mult, [("oc", 0, dv), ("lc", 0)], [("ab", 0)])
                    tt(P, "pool", AB[:, 1, 0:nq], Oc[:, 2 + dv, 0:nq], Lc[:, 1, 0:nq], ALU.mult, [("oc", 1, dv), ("lc", 1)], [("ab", 1)])
                    stt(P, Oo[:, dv, 0:nq], AB[:, 1, 0:nq], neglam, AB[:, 0, 0:nq], ALU.mult, ALU.add,
                        [("ab", 0), ("ab", 1)], [("o", dv)])
                    tt(P, "pool", Sq[:, dv, 0:nq], Oo[:, dv, 0:nq], Oo[:, dv, 0:nq], ALU.mult, [("o", dv)], [("sq", dv)])
                pend.append((h, tok0, nq, is_ctx))
        if pend:
            epi2(*pend.pop())
        P.flush()


GL = 16
N_BAND = 8


def stage_poolmix(B, layer):
    nc, P, cfg = B.nc, B.P, B.cfg
    a = layer // 2
    ctx_out = layer == 1
    NOWN, NTO = cfg.NOWN, cfg.NTO
    U = B.t_u()
    gt, zt = B.t_gt(), B.t_zt()
    grp_w = B.dram("pool_grp_w", [2, 4, 512, 512], F32)[a]
    pscale = B.dram("pool_scale", [2, D], F32)[a]
    bands = B.dram("bands", [4, 9, 128, 128], F32)
    A = Alloc(nc)
    bd = A.sb("m_bd", [128, 4, 9, 128], BF16)
    gw = A.sb("m_gw", [128, 4, 4, 512], BF16)
    psc = A.sb("m_ps", [128, KC], F32)
    us = A.sb("m_u", [128, 6, D], BF16)
    pl = A.sb("m_pl", [128, KC, 512], BF16)
    gs = A.sb("m_g", [128, KC, 512], BF16)
    ys = A.sb("m_y", [128, 2, 512], F32)
    zs = A.sb("m_z", [128, KC, 512], BF16)
    p0 = A.ps("m_p0", [128, 512], F32)
    p1 = A.ps("m_p1", [128, 512], F32)
    p2 = A.ps("m_p2", [128, 512], F32)
    p3 = A.ps("m_p3", [128, 512], F32)
    with A:
        pp = [p0, p1, p2, p3]
        P.dma("pool", bd[:], bands.rearrange("g v p n -> p g v n"), w=["bd"])
        for g in range(4):
            P.dma("pool", gw[:, g, :, :], grp_w[g].rearrange("(c p) e -> p c e", p=128), w=[("gw", g)])
        load_featmajor_vec(P, "sp", psc[:], pscale, ["psc"])
        P.flush()
        chunks = [("own", t0, min(4, NTO - t0)) for t0 in range(0, NTO, 4)]
        if ctx_out:
            chunks.append(("ctx", 0, 2))
        cnt = {"p": 0, "y": 0}
        for (kind, t0, nt) in chunks:
            nq = nt * 128
            if kind == "own":
                prev_row = (t0 - 1) * 128 if t0 > 0 else NOWN
                next_row = (t0 + nt) * 128 if t0 + nt < NTO else NOWN
                P.dma("sp", us[:, 0, :], U[prev_row:prev_row + 128, :], w=[("u", 0)])
                P.dma("pool", us[:, 1:1 + nt, :], U[t0 * 128:(t0 + nt) * 128, :].rearrange("(t p) n -> p t n", p=128),
                      w=[("u", 1 + i) for i in range(nt)])
                P.dma("sp", us[:, 1 + nt, :], U[next_row:next_row + 128, :], w=[("u", 1 + nt)])
                gsrc, zdst, tok0 = gt, zt, t0 * 128
            else:
                base = NOWN + 128
                P.dma("pool", us[:, 1:3, :], U[base:base + 256, :].rearrange("(t p) n -> p t n", p=128), w=[("u", 1), ("u", 2)])
                gsrc, zdst, tok0 = B.t_gtc(), B.t_ztc(), 0
            P.dma("sp", gs[:, :, 0:nq], gsrc[:, :, tok0:tok0 + nq].rearrange("c p n -> p c n"), w=["gs"])
            for ch in range(KC):
                g = ch // 4
                pi = cnt["p"] % 4
                cnt["p"] += 1
                for j in range(nt):
                    if kind == "own":
                        tglob = t0 + j
                        terms = []
                        terms.append((j, 0 if tglob == 0 else 1))
                        terms.append((j + 1, 2 if tglob == 0 else (4 if tglob == NTO - 1 else 3)))
                        terms.append((j + 2, 6 if tglob == NTO - 1 else 5))
                    else:
                        terms = [(1, 7), (2, 5)] if j == 0 else [(1, 1), (2, 8)]
                    for n, (slot, var) in enumerate(terms):
                        mm(P, pp[pi][:, j * 128:(j + 1) * 128], us[:, slot, ch * 128:(ch + 1) * 128], bd[:, g, var, :],
                           n == 0, n == len(terms) - 1, [("u", slot)], [("p", pi)])
                if ch % 2 == 0:
                    act(P, pl[:, ch, 0:nq], pp[pi][:, 0:nq], AF.Copy, [("p", pi)], [("pl", ch)])
                else:
                    cp(P, "dve", pl[:, ch, 0:nq], pp[pi][:, 0:nq], [("p", pi)], [("pl", ch)])
            for g in range(4):
                for ec in range(4):
                    pi = cnt["p"] % 4
                    cnt["p"] += 1
                    for cc in range(4):
                        mm(P, pp[pi][:, 0:nq], gw[:, g, cc, ec * 128:(ec + 1) * 128], pl[:, g * 4 + cc, 0:nq], cc == 0, cc == 3,
                           [("pl", g * 4 + cc)], [("p", pi)])
                    e = g * 4 + ec
                    stt(P, zs[:, e, 0:nq], pp[pi][:, 0:nq], psc[:, e:e + 1], gs[:, e, 0:nq], ALU.mult, ALU.mult,
                        [("p", pi), "gs"], [("z", e)])
            P.dma("sp", zdst[:, :, tok0:tok0 + nq].rearrange("c p n -> p c n"), zs[:, :, 0:nq], r=[("z", e) for e in range(KC)])
        P.flush()


def stage_out(B, layer):
    nc, P, cfg = B.nc, B.P, B.cfg
    is_attn = layer % 2 == 0
    a = layer // 2
    ctx_out = layer in (0, 1)
    NOWN, NTO = cfg.NOWN, cfg.NTO
    w_out = (B.dram("attn_w_out", [2, D, D], F32) if is_attn else B.dram("pool_w_out", [2, D, D], F32))[a]
    modv = B.t_modv()
    ln_g = B.dram("ln_g", [DEPTH, D], F32)
    ln_b = B.dram("ln_b", [DEPTH, D], F32)
    x_src, x_dst = B.t_x(layer), B.t_x(layer + 1)
    zt = B.t_zt()
    groups = [("own", t) for t in range(NTO)]
    if ctx_out:
        groups += [("ctx", 0), ("ctx", 1)]
    A = Alloc(nc)
    wo = A.sb("o_w", [128, KC, D], BF16)
    gate = A.sb("o_gate", [128, 2, D], F32)
    lng = A.sb("o_lng", [128, D], F32)
    lnb = A.sb("o_lnb", [128, D], F32)
    z0 = A.sb("o_z0", [128, KC, 128], BF16)
    z1 = A.sb("o_z1", [128, KC, 128], BF16)
    x0 = A.sb("o_x0", [128, D], F32)
    x1 = A.sb("o_x1", [128, D], F32)
    r0 = A.sb("o_r0", [128, D], F32)
    r1 = A.sb("o_r1", [128, D], F32)
    stats = A.sb("o_st", [128, 2, 24], F32)
    mv = A.sb("o_mv", [128, 2, 4], F32)
    pA = A.ps("o_p0", [128, D], F32)
    pB = A.ps("o_p1", [128, D], F32)
    with A:
        zz, xx, rr, pps = [z0, z1], [x0, x1], [r0, r1], [pA, pB]
        for kq in range(4):
            P.dma("pool", wo[:, kq * 4:(kq + 1) * 4, :], w_out[kq * 512:(kq + 1) * 512, :].rearrange("(k p) n -> p k n", p=128), w=[("wo", kq)])
        for who in range(2):
            P.dma("sp", gate[:, who, :], modv[layer, who:who + 1, 2 * D:3 * D].partition_broadcast(128), w=[("gate", who)])
        P.dma("sp", lng[:], ln_g[layer:layer + 1, :].partition_broadcast(128), w=["lng"])
        P.dma("sp", lnb[:], ln_b[layer:layer + 1, :].partition_broadcast(128), w=["lnb"])
        P.op("dve", lambda e: e.memset(mv[:, :, 3:4], LN_EPS), (), ["eps"])
        P.flush()
        for n, (kind, t) in enumerate(groups):
            s = n % 2
            who = 1 if kind == "ctx" else 0
            if kind == "own":
                zsrc = zt[:, :, t * 128:(t + 1) * 128]
                xs_ap = x_src[t * 128:(t + 1) * 128, :]
                xd_ap = x_dst[t * 128:(t + 1) * 128, :]
            else:
                zsrc = B.t_ztc()[:, :, t * 128:(t + 1) * 128]
                xs_ap = B.t_c(layer)[t * 128:(t + 1) * 128, :]
                xd_ap = B.t_c(layer + 1)[t * 128:(t + 1) * 128, :]
            P.dma("sp", zz[s][:], zsrc.rearrange("c p n -> p c n"), w=[("z", s)])
            P.dma("pool", xx[s][:], xs_ap, w=[("x", s)])
            for nb in range(4):
                for k in range(KC):
                    mm(P, pps[s][:, nb * 512:(nb + 1) * 512], zz[s][:, k, :], wo[:, k, nb * 512:(nb + 1) * 512], k == 0, k == KC - 1,
                       [("z", s)], [("p", s, nb)])
            for nb in range(4):
                sl = slice(nb * 512, (nb + 1) * 512)
                tt(P, "dve", rr[s][:, sl], pps[s][:, sl], gate[:, who, sl], ALU.mult, [("p", s, nb)], [("r", s, nb)])
                stt(P, rr[s][:, sl], xx[s][:, sl], ALU_ALPHA, rr[s][:, sl], ALU.mult, ALU.add, [("x", s), ("r", s, nb)], [("r", s, nb)])
                P.op("dve", (lambda e, s=s, nb=nb, sl=sl: e.bn_stats(out=stats[:, s, nb * 6:(nb + 1) * 6], in_=rr[s][:, sl])),
                     [("r", s, nb)], [("st", s, nb)])
            P.op("dve", (lambda e, s=s: e.bn_aggr(out=mv[:, s, 0:2], in_=stats[:, s, :])),
                 [("st", s, nb) for nb in range(4)], [("mv", s)])
            act(P, mv[:, s, 2:3], mv[:, s, 1:2], AF.Sqrt, [("mv", s)], [("rstd", s)], bias=mv[:, s, 3:4])
            recip(P, mv[:, s, 2:3], mv[:, s, 2:3], [("rstd", s)], [("rstd", s)])
            rkeys = [("r", s, nb) for nb in range(4)]
            ts(P, "dve", rr[s][:], rr[s][:], mv[:, s, 0:1], mv[:, s, 2:3], ALU.subtract, ALU.mult, rkeys + [("mv", s), ("rstd", s)], rkeys)
            tt(P, "pool", rr[s][:], rr[s][:], lng[:], ALU.mult, rkeys, rkeys)
            tt(P, "pool", rr[s][:], rr[s][:], lnb[:], ALU.add, rkeys, rkeys)
            P.dma("sp", xd_ap, rr[s][:], r=rkeys)
        P.flush()


ALU_ALPHA = float(ALPHA)

RG = [[0, 1, 2, 3], [4, 5, 6, 7]]
_ccuid = [0]


def collective_allgather(B, pairs):
    nc = B.nc
    B.P.flush()
    _ccuid[0] += 1
    cm = nc.semaphore("cc%d" % _ccuid[0])
    sm = cm.__enter__()
    B.P._cms.append(cm)
    with nc.Block() as block:
        def body(g):
            for (i, o) in pairs:
                g.collective_compute("AllGather", ALU.bypass, replica_groups=RG, ins=[i.opt()], outs=[o.opt()]).then_inc(sm, 1)
            g.wait_ge(sm, len(pairs))
            g.nop()
        block.gpsimd(body)


def stage_exch_kv(B):
    c = B.cfg
    kt_own, kt_all, v_own, v_all = B.t_kt_own(), B.t_kt_all(), B.t_v_own(), B.t_v_all()
    pairs = []
    for hc in range(16):
        pairs.append((kt_own[hc], kt_all[hc].rearrange("r p n -> (r p) n")))
    for vcn in range(c.NVC):
        pairs.append((v_own[vcn * c.VCH:(vcn + 1) * c.VCH, :], v_all[vcn].rearrange("r t n -> (r t) n")))
    collective_allgather(B, pairs)


def stage_exch_halo(B, layer):
    nc, P, cfg = B.nc, B.P, B.cfg
    NOWN = cfg.NOWN
    x_src = B.t_x(layer)
    xedge = B.dram("xedge", [16, D], F32)
    xedge_all = B.dram("xedge_all", [64, D], F32)
    sel = B.dram("halo_sel", [64, 128], F32)
    xhalo = B.t_xhalo()
    P.dma("sp", xedge[0:8, :], x_src[0:8, :])
    P.dma("sp", xedge[8:16, :], x_src[NOWN - 8:NOWN, :])
    collective_allgather(B, [(xedge, xedge_all)])
    A = Alloc(nc)
    E = A.sb("h_e", [64, D], F32)
    S = A.sb("h_s", [64, 128], F32)
    Hs = A.sb("h_h", [128, D], F32)
    ps = A.ps("h_ps", [128, D], F32)
    with A:
        P.dma("sp", E[:], xedge_all, w=["E"])
        P.dma("sp", S[:], sel, w=["S"])
        for nb in range(4):
            sl = slice(nb * 512, (nb + 1) * 512)
            mm(P, ps[:, sl], S[:], E[:, sl], True, True, ["E", "S"], [("ps", nb)])
            act(P, Hs[:, sl], ps[:, sl], AF.Copy, [("ps", nb)], [("h", nb)])
        P.dma("sp", xhalo, Hs[:], r=[("h", nb) for nb in range(4)])
        P.flush()


def _band(in_off, out_off, S_seq, w):
    lo = w // 2
    hi = w - 1 - lo
    t_out = out_off + np.arange(128)
    t_in = in_off + np.arange(128)
    cnt = (np.minimum(t_out + hi + 1, S_seq) - np.maximum(t_out - lo, 0)).astype(np.float64)
    cnt = np.maximum(cnt, 1.0)
    inwin = (t_in[:, None] >= t_out[None, :] - lo) & (t_in[:, None] <= t_out[None, :] + hi)
    inwin &= (t_in[:, None] >= 0) & (t_in[:, None] < S_seq)
    m = inwin / cnt[None, :] - (t_in[:, None] == t_out[None, :])
    valid_out = (t_out >= 0) & (t_out < S_seq)
    m = m * valid_out[None, :]
    return m.astype(np.float32)


def _consts(cfg, j):
    S, NOWN = cfg.S, cfg.NOWN
    own0, own1 = j * NOWN, (j + 1) * NOWN
    BIG = 1 << 20
    mid = 1 << 10
    bands = np.zeros((4, 9, 128, 128), np.float32)
    for g, w in enumerate(POOL_WINDOWS):
        bands[g, 0] = _band(own0 - 128, own0, S, w)
        bands[g, 1] = _band(mid * 128 - 128, mid * 128, BIG, w)
        bands[g, 2] = _band(own0, own0, S, w)
        bands[g, 3] = _band(mid * 128, mid * 128, BIG, w)
        bands[g, 4] = _band(own1 - 128, own1 - 128, S, w)
        bands[g, 5] = _band(mid * 128 + 128, mid * 128, BIG, w)
        bands[g, 6] = _band(own1, own1 - 128, S, w)
        bands[g, 7] = _band(0, 0, CTX, w)
        bands[g, 8] = _band(128, 128, CTX, w)
    t = (own0 + np.arange(NOWN)).astype(np.float32)
    t_row = np.floor(t / 64.0).astype(np.float32)
    t_col = (t - t_row * 64.0).astype(np.float32)
    inv_freq = (np.float32(10000.0) ** (-np.arange(32, dtype=np.float32) / np.float32(32))).astype(np.float32)
    ang_r = (t_row[None, :] * inv_freq[:, None]).astype(np.float32)
    ang_c = (t_col[None, :] * inv_freq[:, None]).astype(np.float32)
    cr, sr, cc_, sc_ = np.cos(ang_r), np.sin(ang_r), np.cos(ang_c), np.sin(ang_c)
    rope_c = np.concatenate([cr, cr, cc_, cc_], axis=0).astype(np.float32)
    rope_s = np.concatenate([sr, -sr, sc_, -sc_], axis=0).astype(np.float32)
    sel = np.zeros((64, 128), np.float32)
    for i in range(8):
        if j > 0:
            sel[(j - 1) * 16 + 8 + i, 120 + i] = 1.0
        if j < 3:
            sel[(j + 1) * 16 + i, i] = 1.0
    return {"bands": bands, "rope_c": np.ascontiguousarray(rope_c), "rope_s": np.ascontiguousarray(rope_s),
            "ident": np.eye(128, dtype=np.float32), "halo_sel": sel}


STAGES = {
    "mod": (stage_mod, None), "proj": stage_proj, "attn": stage_attn, "poolmix": stage_poolmix, "out": stage_out,
}

ALL_NAMES = ["x0", "x1", "x2", "x3", "c0", "c1", "c2", "modv", "QT", "GT", "QTc", "GTc", "KTc", "Vc", "KT_own", "V_own",
             "KT_all", "V_all", "U", "ZT", "ZTc", "xhalo", "cvec", "mod_w", "mod_b", "ln_g", "ln_b", "attn_w_in", "attn_w_out",
             "attn_lq1", "attn_lk1", "attn_lq2", "attn_lk2", "attn_subln_g", "pool_w_in", "pool_grp_w", "pool_scale",
             "pool_w_out", "ident", "rope_c", "rope_s", "bands", "halo_sel"]


def _produced(stage):
    name, layer = stage
    if name == "mod":
        return {"modv"}
    if name == "proj":
        return {"QT", "GT", "QTc", "GTc", "KTc", "Vc", "KT_own", "V_own"} if layer % 2 == 0 else {"U", "GT", "GTc"}
    if name == "attn":
        return {"ZT", "ZTc", "KT_all", "V_all"}
    if name == "poolmix":
        return {"ZT", "ZTc"}
    if name == "out":
        return {"x%d" % (layer + 1), "c%d" % (layer + 1)}
    if name == "exch_kv":
        return {"KT_all", "V_all"}
    if name == "exch_halo":
        return {"xhalo"}
    return set()


def _uses(B, stage):
    name, layer = stage
    c = B.cfg
    if name == "mod":
        B.dram("cvec", [2, D], F32); B.dram("mod_w", [DEPTH, D, 3 * D // 4], F32); B.dram("mod_b", [DEPTH, 3 * D // 4], F32); B.t_modv()
        B.dram("modp", [DEPTH * 2, 3 * D // 4], F32); B.dram("modg", [4 * DEPTH * 2, 3 * D // 4], F32)
    elif name == "proj":
        if layer % 2 == 0:
            B.dram("attn_w_in", [2, D, 4 * D], F32)
        else:
            B.dram("pool_w_in", [2, D, 2 * D], F32)
        B.t_modv(); B.dram("ident", [128, 128], F32); B.dram("rope_c", [128, c.NOWN], F32); B.dram("rope_s", [128, c.NOWN], F32)
        B.t_x(layer)
        if layer <= 2:
            B.t_c(layer)
        if layer % 2 == 0:
            B.t_qt(); B.t_gt(); B.t_qtc(); B.t_gtc(); B.t_ktc(); B.t_vc(); B.t_kt_own(); B.t_v_own()
        else:
            B.t_xhalo(); B.t_u(); B.t_gt(); B.t_gtc()
    elif name == "attn":
        B.t_qt(); B.t_gt(); B.t_ktc(); B.t_vc(); B.t_kt_all(); B.t_v_all(); B.t_zt(); B.t_kt_own(); B.t_v_own()
        for n in ("attn_lq1", "attn_lk1", "attn_lq2", "attn_lk2"):
            B.dram(n, [2, 128], F32)
        B.dram("attn_subln_g", [2, 256], F32)
        if layer == 0:
            B.t_qtc(); B.t_gtc(); B.t_ztc()
    elif name == "poolmix":
        B.t_u(); B.t_gt(); B.t_zt(); B.dram("pool_grp_w", [2, 4, 512, 512], F32); B.dram("pool_scale", [2, D], F32)
        B.dram("bands", [4, 9, 128, 128], F32)
        if layer == 1:
            B.t_gtc(); B.t_ztc()
    elif name == "exch_kv":
        B.t_kt_own(); B.t_kt_all(); B.t_v_own(); B.t_v_all()
    elif name == "exch_halo":
        B.t_x(layer); B.dram("xedge", [16, D], F32); B.dram("xedge_all", [64, D], F32); B.dram("halo_sel", [64, 128], F32); B.t_xhalo()
    elif name == "out":
        B.dram("attn_w_out" if layer % 2 == 0 else "pool_w_out", [2, D, D], F32)
        B.t_modv(); B.dram("ln_g", [DEPTH, D], F32); B.dram("ln_b", [DEPTH, D], F32)
        B.t_x(layer); B.t_x(layer + 1); B.t_zt()
        if layer in (0, 1):
            B.t_ztc(); B.t_c(layer); B.t_c(layer + 1)


def build_launch(cfg, stages, outs):
    produced = set()
    for st in stages:
        produced |= _produced(st)
    ext_in = [n for n in ALL_NAMES if n not in produced]
    B = Build(cfg, ext_in, outs)
    for st in stages:
        _uses(B, st)
    for (name, layer) in stages:
        if name == "mod":
            stage_mod(B)
        elif name == "proj":
            stage_proj(B, layer)
        elif name == "attn":
            stage_attn(B, layer)
        elif name == "poolmix":
            stage_poolmix(B, layer)
        elif name == "out":
            stage_out(B, layer)
        elif name == "exch_kv":
            stage_exch_kv(B)
        elif name == "exch_halo":
            stage_exch_halo(B, layer)
        else:
            raise ValueError(name)
    done = B.nc.dram_tensor("done", [1, 16], F32, kind="ExternalOutput").ap()
    B.used_out.append("done")
    A = Alloc(B.nc)
    dn = A.sb("dn", [1, 16], F32)
    with A:
        B.P.op("dve", lambda e: e.memset(dn[:], 1.0), (), ["dn"])
        B.P.dma("sp", done, dn[:], r=["dn"])
        B.P.flush()
    B.P.close()
    return B


def run_launch(cfg, stages, outs, pool):
    B = build_launch(cfg, stages, outs)
    def _in(a):
        a = np.asarray(a)
        return a.view(np.uint16) if a.dtype == ml_dtypes.bfloat16 else a
    in_maps = [{n: _in(pool[r][n]) for n in B.used_in} for r in range(8)]
    res = run_bass_kernel_spmd(B.nc, in_maps, core_ids=list(range(8)))
    for r in range(8):
        for n in B.used_out:
            a = np.asarray(res.results[r][n])
            pool[r][n] = a.view(ml_dtypes.bfloat16) if a.dtype == np.uint16 else a
    return B


def _host_exchange_kv(pool):
    for b in range(2):
        kt = np.ascontiguousarray(np.stack([pool[b * 4 + r]["KT_own"] for r in range(4)], axis=1))
        v = np.stack([pool[b * 4 + r]["V_own"] for r in range(4)], axis=0)
        nown = v.shape[1]
        vch = min(256, nown)
        v = np.ascontiguousarray(v.reshape(4, nown // vch, vch, D).transpose(1, 0, 2, 3))
        for r in range(4):
            pool[b * 4 + r]["KT_all"] = kt
            pool[b * 4 + r]["V_all"] = v


def _host_exchange_halo(pool, name):
    for b in range(2):
        for j in range(4):
            hal = np.zeros((128, D), np.float32)
            if j > 0:
                hal[120:128] = pool[b * 4 + j - 1][name][-8:]
            if j < 3:
                hal[0:8] = pool[b * 4 + j + 1][name][:8]
            pool[b * 4 + j]["xhalo"] = hal


def make_pool(cfg, inputs):
    f = lambda a: np.ascontiguousarray(np.asarray(a, dtype=np.float32))
    x, c, ctx, c_ctx = f(inputs["x"]), f(inputs["c"]), f(inputs["ctx"]), f(inputs["c_ctx"])
    mod_w_full, mod_b_full = f(inputs["mod_w"]), f(inputs["mod_b"])
    MC = 3 * D // 4
    shared = {k: f(inputs[k]) for k in ("ln_g", "ln_b", "attn_w_in", "attn_w_out", "attn_lq1", "attn_lk1",
                                         "attn_lq2", "attn_lk2", "attn_subln_g", "pool_w_in", "pool_grp_w", "pool_scale",
                                         "pool_w_out")}
    pool = []
    for r in range(8):
        b, j = r // 4, r % 4
        d = dict(shared)
        d["x0"] = np.ascontiguousarray(x[b, j * cfg.NOWN:(j + 1) * cfg.NOWN])
        d["c0"] = np.ascontiguousarray(ctx[b])
        d["cvec"] = np.ascontiguousarray(np.stack([c[b], c_ctx], axis=0))
        d["mod_w"] = np.ascontiguousarray(mod_w_full[:, :, j * MC:(j + 1) * MC])
        d["mod_b"] = np.ascontiguousarray(mod_b_full[:, j * MC:(j + 1) * MC])
        d.update(_consts(cfg, j))
        pool.append(d)
    return pool


def run_multi(inputs, nl=DEPTH):
    S = int(np.asarray(inputs["x"]).shape[1])
    cfg = Cfg(S)
    pool = make_pool(cfg, inputs)
    kvo = ["modv", "QT", "GT", "QTc", "GTc", "KTc", "Vc", "KT_own", "V_own"]
    run_launch(cfg, [("mod", 0), ("proj", 0)], kvo, pool)
    _host_exchange_kv(pool)
    run_launch(cfg, [("attn", 0), ("out", 0)], ["x1", "c1"], pool)
    if nl >= 2:
        _host_exchange_halo(pool, "x1")
        st = [("proj", 1), ("poolmix", 1), ("out", 1)]
        outs = ["x2", "c2"]
        if nl >= 3:
            st.append(("proj", 2))
            outs += kvo[1:]
        run_launch(cfg, st, outs, pool)
    if nl >= 3:
        _host_exchange_kv(pool)
        run_launch(cfg, [("attn", 2), ("out", 2)], ["x3"], pool)
    if nl >= 4:
        _host_exchange_halo(pool, "x3")
        run_launch(cfg, [("proj", 3), ("poolmix", 3), ("out", 3)], ["x4"], pool)
    name = "x%d" % nl
    out = np.stack([np.concatenate([pool[b * 4 + j][name] for j in range(4)], axis=0) for b in range(2)], axis=0)
    return out.astype(np.float32)


FUSED_STAGES = [("mod", 0), ("proj", 0), ("attn", 0), ("out", 0),
                ("exch_halo", 1), ("proj", 1), ("poolmix", 1), ("out", 1),
                ("proj", 2), ("attn", 2), ("out", 2),
                ("exch_halo", 3), ("proj", 3), ("poolmix", 3), ("out", 3)]


def run_fused(inputs):
    S = int(np.asarray(inputs["x"]).shape[1])
    cfg = Cfg(S)
    pool = make_pool(cfg, inputs)
    run_launch(cfg, FUSED_STAGES, ["x4"], pool)
    out = np.stack([np.concatenate([pool[b * 4 + j]["x4"] for j in range(4)], axis=0) for b in range(2)], axis=0)
    return out.astype(np.float32)


def kernel(**inputs):
    return run_fused(inputs)
```

```python
import math
from contextlib import ExitStack
import numpy as np
import ml_dtypes
import concourse.bass as bass
import concourse.mybir as mybir
from concourse.bass_utils import run_bass_kernel_spmd

F32 = mybir.dt.float32
BF16 = mybir.dt.bfloat16
AF = mybir.ActivationFunctionType
ALU = mybir.AluOpType
AX = mybir.AxisListType

D = 2048
KC = 16
H = 8
CTX = 256
DEPTH = 4
ALPHA = (2 * DEPTH) ** 0.25
LN_EPS = 1e-5
SUBLN_EPS = 1e-5
POOL_WINDOWS = (2, 4, 8, 16)
NSD = 12
DBG = {"proj_stop": 99, "v_step": 9, "oq": "sp", "a1": 9}


class Op:
    __slots__ = ("eng", "fn", "deps", "sig", "sigval", "dma", "dsem", "dval", "prevdma", "xw", "cinc")

    def __init__(self, eng, fn, dma):
        self.eng = eng
        self.fn = fn
        self.dma = dma
        self.deps = ()
        self.sig = False
        self.sigval = 0
        self.dsem = None
        self.dval = 0
        self.prevdma = None
        self.xw = ()
        self.cinc = None


class Prog:
    ENG = ("pe", "act", "dve", "pool", "sp")
    QS = ("sp", "act", "pool")

    def __init__(self, nc, sync_same_engine=True):
        self.nc = nc
        self.sync_same = sync_same_engine
        self._cms = []
        self.sems = {}
        for e in self.ENG:
            cm = nc.semaphore("s_" + e)
            self.sems[e] = cm.__enter__()
            self._cms.append(cm)
        self.dsems = {}
        for q in self.QS:
            lst = []
            for i in range(NSD):
                cm = nc.semaphore("d_%s%d" % (q, i))
                lst.append(cm.__enter__())
                self._cms.append(cm)
            self.dsems[q] = lst
        self.sigcount = {e: 0 for e in self.ENG}
        self.dcount = {q: 0 for q in self.QS}
        self.dhist = {q: [] for q in self.QS}
        self.seen = {e: {} for e in self.ENG}
        self.nops = 0
        self._reset()

    def _reset(self):
        self.ops = {e: [] for e in self.ENG}
        self.last_w = {}
        self.readers = {}

    def close(self):
        for cm in reversed(self._cms):
            cm.__exit__(None, None, None)

    def _add(self, op, reads, writes):
        deps = set()
        for b in reads:
            w = self.last_w.get(b)
            if w is not None:
                deps.add(w)
        for b in writes:
            w = self.last_w.get(b)
            if w is not None:
                deps.add(w)
            rs = self.readers.get(b)
            if rs:
                deps.update(rs.values())
        deps.discard(op)
        for b in reads:
            d = self.readers.setdefault(b, {})
            key = (op.eng, id(op)) if op.dma else op.eng
            d[key] = op
        for b in writes:
            self.last_w[b] = op
            self.readers[b] = {}
        fd = []
        for d in deps:
            if (not d.dma) and (not op.dma) and d.eng == op.eng:
                if d.eng == "pe" or not self.sync_same:
                    continue
            if not d.dma:
                d.sig = True
            fd.append(d)
        op.deps = fd
        self.ops[op.eng].append(op)
        self.nops += 1
        return op

    def op(self, eng, fn, r=(), w=()):
        return self._add(Op(eng, fn, False), r, w)

    def dma(self, q, out, in_, r=(), w=(), xw=(), **kw):
        def fn(e):
            return e.dma_start(out=out, in_=in_, **kw)
        op = Op(q, fn, True)
        op.xw = xw
        n = self.dcount[q]
        self.dcount[q] = n + 1
        op.dsem = self.dsems[q][n % NSD]
        op.dval = 16 * (n // NSD + 1)
        h = self.dhist[q]
        if len(h) >= NSD:
            op.prevdma = h[-NSD]
        h.append(op)
        if len(h) > 2 * NSD:
            del h[0:len(h) - 2 * NSD]
        return self._add(op, r, w)

    def flush(self):
        nc = self.nc
        fin = Op("sp", None, False)
        deps = []
        for q in self.QS:
            deps.extend(self.dhist[q][-NSD:])
        for e in self.ENG:
            if e != "sp" and self.ops[e]:
                last = self.ops[e][-1]
                if not last.dma:
                    last.sig = True
                deps.append(last)
        fin.deps = deps
        self.ops["sp"].append(fin)
        for e in self.ENG:
            c = self.sigcount[e]
            for o in self.ops[e]:
                if o.sig and not o.dma:
                    c += 1
                    o.sigval = c
            self.sigcount[e] = c
        handles = {"pe": "tensor", "act": "scalar", "dve": "vector", "pool": "gpsimd", "sp": "sync"}
        with nc.Block() as block:
            for e in self.ENG:
                ops = self.ops[e]
                if not ops:
                    continue

                def body(eng, ops=ops, seen=self.seen[e], sem_e=self.sems[e]):
                    for o in ops:
                        if o.dma and o.prevdma is not None:
                            p = o.prevdma
                            k = id(p.dsem)
                            if seen.get(k, 0) < p.dval:
                                eng.wait_ge(p.dsem, p.dval)
                                seen[k] = p.dval
                        for d in o.deps:
                            if d.dma:
                                s, v = d.dsem, d.dval
                            else:
                                s, v = self.sems[d.eng], d.sigval
                            k = id(s)
                            if seen.get(k, 0) < v:
                                eng.wait_ge(s, v)
                                seen[k] = v
                        for (xs_, xv_) in o.xw:
                            k = id(xs_)
                            if seen.get(k, 0) < xv_:
                                eng.wait_ge(xs_, xv_)
                                seen[k] = xv_
                        if o.fn is None:
                            eng.nop()
                            continue
                        inst = o.fn(eng)
                        if o.cinc is not None:
                            inst.then_inc(o.cinc[0], o.cinc[1])
                        elif o.dma:
                            inst.then_inc(o.dsem, 16)
                        elif o.sig:
                            inst.then_inc(sem_e, 1)

                getattr(block, handles[e])(body)
        self._reset()


class Alloc:
    _uid = [0]

    def __init__(self, nc):
        self.nc = nc
        self.es = ExitStack()
        Alloc._uid[0] += 1
        self.sfx = "_%d" % Alloc._uid[0]

    def sb(self, name, shape, dt):
        return self.es.enter_context(self.nc.sbuf_tensor(name + self.sfx, shape, dt))

    def ps(self, name, shape, dt):
        return self.es.enter_context(self.nc.psum_tensor(name + self.sfx, shape, dt))

    def __enter__(self):
        return self

    def __exit__(self, *a):
        self.es.close()
        return False


def mm(P, out, lhsT, rhs, start, stop, r, w):
    P.op("pe", lambda e: e.matmul(out, lhsT=lhsT, rhs=rhs, start=start, stop=stop), r, w)


def tr(P, out, in_, ident, r, w):
    P.op("pe", lambda e: e.transpose(out=out, in_=in_, identity=ident), r, w)


def act(P, out, in_, func, r, w, scale=None, bias=None):
    kw = {}
    if scale is not None:
        kw["scale"] = scale
    if bias is not None:
        kw["bias"] = bias
    P.op("act", lambda e: e.activation(out=out, in_=in_, func=func, **kw), r, w)


def tt(P, eng, out, in0, in1, op, r, w):
    P.op(eng, lambda e: e.tensor_tensor(out=out, in0=in0, in1=in1, op=op), r, w)


def ts(P, eng, out, in0, s1, s2, op0, op1, r, w):
    if op1 is None:
        P.op(eng, lambda e: e.tensor_scalar(out=out, in0=in0, scalar1=s1, scalar2=None, op0=op0), r, w)
    else:
        P.op(eng, lambda e: e.tensor_scalar(out=out, in0=in0, scalar1=s1, scalar2=s2, op0=op0, op1=op1), r, w)


def stt(P, out, in0, scalar, in1, op0, op1, r, w):
    P.op("dve", lambda e: e.scalar_tensor_tensor(out=out, in0=in0, scalar=scalar, in1=in1, op0=op0, op1=op1), r, w)


def cp(P, eng, out, in_, r, w):
    P.op(eng, lambda e: e.tensor_copy(out=out, in_=in_), r, w)


def recip(P, out, in_, r, w):
    P.op("dve", lambda e: e.reciprocal(out=out, in_=in_), r, w)


class Cfg:
    def __init__(self, S):
        self.S = S
        self.NOWN = S // 4
        self.NTO = self.NOWN // 128
        self.SCT = min(16, self.NTO)
        self.NK = CTX + S
        self.NKT = self.NK // 128
        self.QW = min(512, self.NOWN)
        self.VCH = min(256, self.NOWN)
        self.NVC = self.NOWN // self.VCH


class Build:
    def __init__(self, cfg, ext_in, ext_out):
        self.cfg = cfg
        self.nc = bass.Bass("TRN2", target_bir_lowering=False)
        self.ext_in = set(ext_in)
        self.ext_out = set(ext_out)
        self.P = Prog(self.nc)
        self.tensors = {}
        self.used_in = []
        self.used_out = []

    def dram(self, name, shape, dt):
        if name in self.tensors:
            return self.tensors[name]
        if name in self.ext_in:
            kind = "ExternalInput"
            self.used_in.append(name)
        elif name in self.ext_out:
            kind = "ExternalOutput"
            self.used_out.append(name)
        else:
            kind = "Internal"
        if False and kind != "Internal" and dt == BF16:
            t = self.nc.dram_tensor(name, list(shape), mybir.dt.uint16, kind=kind).ap().bitcast(BF16)
        else:
            t = self.nc.dram_tensor(name, list(shape), dt, kind=kind).ap()
        self.tensors[name] = t
        return t

    def t_x(self, i):
        c = self.cfg
        return self.dram("x%d" % i, [c.NOWN, D], F32)

    def t_c(self, i):
        return self.dram("c%d" % i, [CTX, D], F32)

    def t_modv(self):
        return self.dram("modv", [DEPTH, 2, 3 * D], F32)

    def t_qt(self):
        return self.dram("QT", [16, 128, self.cfg.NOWN], BF16)

    def t_gt(self):
        return self.dram("GT", [16, 128, self.cfg.NOWN], BF16)

    def t_qtc(self):
        return self.dram("QTc", [16, 128, CTX], BF16)

    def t_gtc(self):
        return self.dram("GTc", [16, 128, CTX], BF16)

    def t_ktc(self):
        return self.dram("KTc", [16, 128, CTX], BF16)

    def t_vc(self):
        return self.dram("Vc", [CTX, D], BF16)

    def t_kt_own(self):
        return self.dram("KT_own", [16, 128, self.cfg.NOWN], BF16)

    def t_v_own(self):
        return self.dram("V_own", [H, self.cfg.NOWN, 256], BF16)

    def t_kt_all(self):
        return self.dram("KT_all", [16, 4, 128, self.cfg.NOWN], BF16)

    def t_v_all(self):
        c = self.cfg
        return self.dram("V_all", [H, 2, 4, c.NOWN // 2, 256], BF16)

    def t_u(self):
        return self.dram("U", [self.cfg.NOWN + 128 + CTX, D], BF16)

    def t_zt(self):
        return self.dram("ZT", [16, 128, self.cfg.NOWN], BF16)

    def t_ztc(self):
        return self.dram("ZTc", [16, 128, CTX], BF16)

    def t_xhalo(self):
        return self.dram("xhalo", [128, D], F32)


def load_featmajor_vec(P, q, out_sb, dram_vec, w):
    P.dma(q, out_sb, dram_vec.rearrange("(k p) -> p k", p=128), w=w, allow_slow_non_contiguous=True)


def stage_mod(B):
    nc, P = B.nc, B.P
    MC = 3 * D // 4
    cvec = B.dram("cvec", [2, D], F32)
    mod_w = B.dram("mod_w", [DEPTH, D, MC], F32)
    mod_b = B.dram("mod_b", [DEPTH, MC], F32)
    modv = B.t_modv()
    modp = B.dram("modp", [DEPTH * 2, MC], F32)
    modg = B.dram("modg", [4 * DEPTH * 2, MC], F32)
    A = Alloc(nc)
    cv = A.sb("m_cv", [128, KC, 2], F32)
    cvl = A.sb("m_cvl", [128, 2, KC], F32)
    w0 = A.sb("m_w0", [128, KC, 512], F32)
    w1 = A.sb("m_w1", [128, KC, 512], F32)
    bb = A.sb("m_b", [2, MC], F32)
    row = A.sb("m_row", [2, MC], F32)
    ps0 = A.ps("m_ps0", [128, 512], F32)
    ps1 = A.ps("m_ps1", [128, 512], F32)
    with A:
        wb = [w0, w1]
        pss = [ps0, ps1]
        for who in range(2):
            P.dma("sp", cvl[:, who, :], cvec[who].rearrange("(k p) -> p k", p=128), w=[("cvl", who)], allow_slow_non_contiguous=True)
            act(P, cvl[:, who, :], cvl[:, who, :], AF.Silu, [("cvl", who)], [("cvl", who)])
            cp(P, "dve", cv[:, :, who], cvl[:, who, :], [("cvl", who)], ["cv"])
        n = 0
        for i in range(DEPTH):
            P.dma("sp", bb[:], mod_b[i:i + 1, :].partition_broadcast(2), r=(), w=["bb"])
            for nb in range(MC // 512):
                s = n % 2
                q = "sp" if n % 2 == 0 else "pool"
                P.dma(q, wb[s][:], mod_w[i, :, nb * 512:(nb + 1) * 512].rearrange("(k p) n -> p k n", p=128), w=[("w", s)])
                for k in range(KC):
                    mm(P, pss[s][0:2, :], cv[:, k, :], wb[s][:, k, :], k == 0, k == KC - 1, ["cv", ("w", s)], [("ps", s)])
                tt(P, "dve", row[:, nb * 512:(nb + 1) * 512], pss[s][0:2, :], bb[:, nb * 512:(nb + 1) * 512], ALU.add,
                   [("ps", s), "bb"], ["row"])
                n += 1
            P.dma("sp", modp[i * 2:(i + 1) * 2, :], row[:], r=["row"], w=())
        collective_allgather(B, [(modp, modg)])
        for r in range(4):
            P.dma("sp", modv[:, :, r * MC:(r + 1) * MC], modg[r * 8:(r + 1) * 8, :].rearrange("(l w) j -> l w j", w=2))
        P.flush()


def stage_proj(B, layer):
    nc, P, cfg = B.nc, B.P, B.cfg
    is_attn = layer % 2 == 0
    a = layer // 2
    ctx_needed = layer <= 2
    ctx_full = layer in (0, 1)
    if is_attn:
        w_in = B.dram("attn_w_in", [2, D, 4 * D], F32)[a]
        ncols = 4 * D
    else:
        w_in = B.dram("pool_w_in", [2, D, 2 * D], F32)[a]
        ncols = 2 * D
    modv = B.t_modv()
    ident = B.dram("ident", [128, 128], F32)
    rope_c = B.dram("rope_c", [128, cfg.NOWN], F32)
    rope_s = B.dram("rope_s", [128, cfg.NOWN], F32)
    x_src = B.t_x(layer)
    c_src = B.t_c(layer) if ctx_needed else None
    NOWN, NTO, SCT = cfg.NOWN, cfg.NTO, cfg.SCT

    tiles = [("own", t) for t in range(NTO)]
    extra = []
    if not is_attn:
        extra.append(("halo", 0))
    if ctx_needed:
        extra += [("ctx", 0), ("ctx", 1)]
    scs = [tiles[i:i + SCT] for i in range(0, NTO, SCT)]
    scs[-1] = scs[-1] + extra

    def src_ap(kind, t):
        if kind == "own":
            return x_src[t * 128:(t + 1) * 128, :]
        if kind == "halo":
            return B.t_xhalo()[:, :]
        return c_src[t * 128:(t + 1) * 128, :]

    if is_attn:
        blk_kind = ["q"] * 4 + ["k"] * 4 + ["v"] * 4 + ["g"] * 4
    else:
        blk_kind = ["u"] * 4 + ["g"] * 4
    nblk = ncols // 512

    A = Alloc(nc)
    hxT = A.sb("p_hxT", [128, KC, (SCT + 3) * 128], BF16)
    xs0 = A.sb("p_x0", [128, D], F32)
    xs1 = A.sb("p_x1", [128, D], F32)
    wb0 = A.sb("p_w0", [128, KC, 512], BF16)
    wb1 = A.sb("p_w1", [128, KC, 512], BF16)
    rc = A.sb("p_rc", [128, max(SCT, 3) * 128], F32)
    rs = A.sb("p_rs", [128, max(SCT, 3) * 128], F32)
    idt = A.sb("p_id", [128, 128], F32)
    modt = A.sb("p_mod", [128, 2, 2, KC], F32)
    qf = A.sb("p_qf", [128, 2, 512], F32)
    ta = A.sb("p_ta", [128, 2, 512], F32)
    tb = A.sb("p_tb", [128, 2, 512], F32)
    ob = A.sb("p_ob", [128, 4, 512], BF16)
    pt0 = A.ps("p_pt0", [128, 512], F32)
    pt1 = A.ps("p_pt1", [128, 512], F32)
    pa0 = A.ps("p_pa0", [128, 512], F32)
    pa1 = A.ps("p_pa1", [128, 512], F32)
    pa2 = A.ps("p_pa2", [128, 512], F32)
    pa3 = A.ps("p_pa3", [128, 512], F32)
    with A:
        xs = [xs0, xs1]
        wbs = [wb0, wb1]
        ptr = [pt0, pt1]
        pacc = [pa0, pa1, pa2, pa3]
        P.dma("sp", idt[:], ident, w=["id"])
        for who in range(2):
            load_featmajor_vec(P, "sp", modt[:, who, 0, :], modv[layer, who, 0:D], [("mod", who, 0)])
            load_featmajor_vec(P, "sp", modt[:, who, 1, :], modv[layer, who, D:2 * D], [("mod", who, 1)])
            ts(P, "dve", modt[:, who, 1, :], modt[:, who, 1, :], 1.0, None, ALU.add, None, [("mod", who, 1)], [("mod", who, 1)])
        P.flush()

        cnt = {"x": 0, "pt": 0, "w": 0, "pa": 0, "ob": 0, "st": 0}
        if DBG["proj_stop"] <= 0:
            scs = []
        for sc in scs:
            ntile = len(sc)
            ntok = ntile * 128
            n_own = sum(1 for kd, _ in sc if kd == "own")
            if is_attn:
                t0 = sc[0][1]
                P.dma("sp", rc[:, 0:n_own * 128], rope_c[:, t0 * 128:(t0 + n_own) * 128], w=["rc"])
                P.dma("sp", rs[:, 0:n_own * 128], rope_s[:, t0 * 128:(t0 + n_own) * 128], w=["rs"])
            nchunks = [(i0 * 128, min(4, n_own - i0) * 128, True) for i0 in range(0, n_own, 4)]
            if ntile > n_own:
                nchunks.append((n_own * 128, (ntile - n_own) * 128, False))
            for ti, (kind, t) in enumerate(sc):
                who = 1 if kind == "ctx" else 0
                s = cnt["x"] % 2
                cnt["x"] += 1
                P.dma("sp" if s == 0 else "pool", xs[s][:], src_ap(kind, t), w=[("x", s)])
                for g4 in range(4):
                    ps = cnt["pt"] % 2
                    cnt["pt"] += 1
                    if DBG["a1"] < 1:
                        continue
                    for j in range(4):
                        c = g4 * 4 + j
                        tr(P, ptr[ps][:, j * 128:(j + 1) * 128], xs[s][:, c * 128:(c + 1) * 128], idt[:],
                           [("x", s)], [("pt", ps)])
                    if DBG["a1"] < 2:
                        continue
                    for j in range(4):
                        c = g4 * 4 + j
                        ts(P, "dve", hxT[:, c, ti * 128:(ti + 1) * 128], ptr[ps][:, j * 128:(j + 1) * 128],
                           modt[:, who, 1, c:c + 1], modt[:, who, 0, c:c + 1], ALU.mult, ALU.add, [("pt", ps)], [("hx", ti)])
            for cb in range(nblk):
                kind_b = blk_kind[cb]
                if DBG["proj_stop"] <= 1:
                    continue
                if DBG["proj_stop"] == 2 and kind_b not in ("v", "u"):
                    continue
                if DBG["proj_stop"] == 3 and kind_b not in ("v", "u", "g"):
                    continue
                ws = cnt["w"] % 2
                cnt["w"] += 1
                P.dma("pool", wbs[ws][:], w_in[:, cb * 512:(cb + 1) * 512].rearrange("(k p) n -> p k n", p=128), w=[("w", ws)])
                hx_keys = [("hx", ti) for ti in range(ntile)]
                if kind_b in ("v", "u"):
                    for ti, (kind, t) in enumerate(sc):
                        if kind_b == "v" and kind == "halo":
                            continue
                        pa = cnt["pa"] % 4
                        cnt["pa"] += 1
                        if DBG["v_step"] < 1:
                            continue
                        for k in range(KC):
                            mm(P, pacc[pa][:], hxT[:, k, ti * 128:(ti + 1) * 128], wbs[ws][:, k, :], k == 0, k == KC - 1,
                               [("hx", ti), ("w", ws)], [("pa", pa)])
                        if DBG["v_step"] < 2:
                            continue
                        o = cnt["ob"] % 4
                        cnt["ob"] += 1
                        if cnt["st"] % 2 == 0:
                            act(P, ob[:, o, :], pacc[pa][:], AF.Copy, [("pa", pa)], [("ob", o)])
                        else:
                            cp(P, "dve", ob[:, o, :], pacc[pa][:], [("pa", pa)], [("ob", o)])
                        cnt["st"] += 1
                        if DBG["v_step"] < 3:
                            continue
                        cols = slice((cb % 4) * 512, (cb % 4 + 1) * 512)
                        if kind_b == "v":
                            if kind == "own":
                                for hh in range(2):
                                    P.dma(DBG["oq"], B.t_v_own()[(cb % 4) * 2 + hh, t * 128:(t + 1) * 128, :],
                                          ob[:, o, hh * 256:(hh + 1) * 256], r=[("ob", o)])
                                continue
                            dst = B.t_vc()[t * 128:(t + 1) * 128, cols]
                        else:
                            if kind == "own":
                                row0 = t * 128
                            elif kind == "halo":
                                row0 = NOWN
                            else:
                                row0 = NOWN + 128 + t * 128
                            dst = B.t_u()[row0:row0 + 128, cols]
                        P.dma(DBG["oq"], dst, ob[:, o, :], r=[("ob", o)])
                    continue
                for j in range(4):
                    cc = (cb % 4) * 4 + j
                    for (n0, nw, own_sc) in nchunks:
                        if not own_sc:
                            if not any(kd == "ctx" for kd, _ in sc):
                                continue
                            if kind_b in ("q",) and not ctx_full:
                                continue
                            if kind_b == "g" and (not ctx_full) and is_attn:
                                continue
                        pa = cnt["pa"] % 4
                        cnt["pa"] += 1
                        t_lo, t_hi = n0 // 128, (n0 + nw) // 128
                        for k in range(KC):
                            mm(P, pacc[pa][:, 0:nw], wbs[ws][:, k, j * 128:(j + 1) * 128], hxT[:, k, n0:n0 + nw],
                               k == 0, k == KC - 1, hx_keys[t_lo:t_hi] + [("w", ws)], [("pa", pa)])
                        o = cnt["ob"] % 4
                        cnt["ob"] += 1
                        if kind_b == "g":
                            act(P, ob[:, o, 0:nw], pacc[pa][:, 0:nw], AF.Silu, [("pa", pa)], [("ob", o)])
                        elif own_sc and kind_b in ("q", "k"):
                            f = cnt["st"] % 2
                            cnt["st"] += 1
                            act(P, qf[:, f, 0:nw], pacc[pa][:, 0:nw], AF.Copy, [("pa", pa)], [("qf", f)])
                            tt(P, "dve", ta[:, f, 0:nw], qf[:, f, 0:nw], rc[:, n0:n0 + nw], ALU.mult, [("qf", f), "rc"], [("ta", f)])
                            for qd, (o0, i0) in enumerate(((0, 32), (32, 0), (64, 96), (96, 64))):
                                tt(P, "pool", tb[o0:o0 + 32, f, 0:nw], qf[i0:i0 + 32, f, 0:nw], rs[i0:i0 + 32, n0:n0 + nw], ALU.mult,
                                   [("qf", f), "rs"], [("tb", f, qd)])
                            tt(P, "dve", ob[:, o, 0:nw], ta[:, f, 0:nw], tb[:, f, 0:nw], ALU.add,
                               [("ta", f)] + [("tb", f, qd) for qd in range(4)], [("ob", o)])
                        else:
                            act(P, ob[:, o, 0:nw], pacc[pa][:, 0:nw], AF.Copy, [("pa", pa)], [("ob", o)])
                        if own_sc:
                            tok0 = sc[0][1] * 128 + n0
                            dstT = {"q": B.t_qt, "k": B.t_kt_own, "g": B.t_gt}[kind_b]()
                            P.dma("sp", dstT[cc, :, tok0:tok0 + nw], ob[:, o, 0:nw], r=[("ob", o)])
                        else:
                            for ti, (kind, t) in enumerate(sc):
                                if kind != "ctx":
                                    continue
                                dstT = {"q": B.t_qtc, "k": B.t_ktc, "g": B.t_gtc}[kind_b]()
                                P.dma("sp", dstT[cc, :, t * 128:(t + 1) * 128], ob[:, o, ti * 128 - n0:(ti + 1) * 128 - n0], r=[("ob", o)])
            P.flush()


def stage_attn(B, layer):
    nc, P, cfg = B.nc, B.P, B.cfg
    a = layer // 2
    lam_init = 0.8 - 0.6 * math.exp(-0.3 * layer)
    ctx_q = layer == 0
    NOWN, NTO, NKT, QW = cfg.NOWN, cfg.NTO, cfg.NKT, cfg.QW
    NK = cfg.NK
    qt, gt, ktc, vc = B.t_qt(), B.t_gt(), B.t_ktc(), B.t_vc()
    kt_all, v_all = B.t_kt_all(), B.t_v_all()
    zt = B.t_zt()
    lqk = [B.dram(n, [2, 128], F32) for n in ("attn_lq1", "attn_lk1", "attn_lq2", "attn_lk2")]
    subln = B.dram("attn_subln_g", [2, 256], F32)
    scale = 1.0 / math.sqrt(128.0)

    A = Alloc(nc)
    Kt = A.sb("a_kt", [128, 2, NK], BF16)
    Vt = A.sb("a_vt", [128, NKT, 256], BF16)
    Qs = A.sb("a_q", [128, 2, 2, 512], BF16)
    PT = A.sb("a_pt", [128, 4, 2, 512], BF16)
    Pacc = A.sb("a_pacc", [128, 2, 512], F32)
    Paccb = A.sb("a_paccb", [128, 2, 2, 512], BF16)
    ones_b = A.sb("a_ones", [128, 128], BF16)
    ones_f = A.sb("a_onesf", [128, 128], F32)
    lq = A.sb("a_lq", [128, 4, 128], F32)
    sc = A.sb("a_sc", [128, 8], F32)
    sg = A.sb("a_sg", [128, 2], F32)
    Oc = A.sb("a_oc", [128, 4, 512], F32)
    Lc = A.sb("a_lc", [128, 2, 512], F32)
    AB = A.sb("a_ab", [128, 2, 512], F32)
    Ys = AB
    Oo = A.sb("a_o", [128, 2, 512], F32)
    Sq = A.sb("a_sq", [128, 2, 512], F32)
    Rt = A.sb("a_rt", [128, 512], F32)
    Gs = A.sb("a_g", [128, 2, 512], BF16)
    Zs = A.sb("a_z", [128, 2, 512], BF16)
    s0 = A.ps("a_s0", [128, 512], F32)
    s1 = A.ps("a_s1", [128, 512], F32)
    o00 = A.ps("a_o00", [128, 512], F32)
    o01 = A.ps("a_o01", [128, 512], F32)
    o10 = A.ps("a_o10", [128, 512], F32)
    o11 = A.ps("a_o11", [128, 512], F32)
    l0 = A.ps("a_l0", [128, 512], F32)
    l1 = A.ps("a_l1", [128, 512], F32)
    with A:
        Sps = [s0, s1]
        Ops = [[o00, o01], [o10, o11]]
        Lps = [l0, l1]
        P.op("dve", lambda e: e.memset(ones_b[:], 1.0), (), ["c_ones"])
        P.op("dve", lambda e: e.memset(ones_f[:], 1.0), (), ["c_onesf"])
        P.op("dve", lambda e: e.memset(sc[:, 6:7], SUBLN_EPS), (), ["c_eps"])
        for i in range(4):
            P.dma("sp", lq[:, i, :], lqk[i][a:a + 1, :].partition_broadcast(128), w=[("lq", i)])
        tt(P, "dve", lq[:, 0, :], lq[:, 0, :], lq[:, 1, :], ALU.mult, [("lq", 0), ("lq", 1)], [("lq", 0)])
        tt(P, "dve", lq[:, 2, :], lq[:, 2, :], lq[:, 3, :], ALU.mult, [("lq", 2), ("lq", 3)], [("lq", 2)])
        P.op("dve", lambda e: e.tensor_reduce(out=sc[:, 0:1], in_=lq[:, 0, :], axis=AX.X, op=ALU.add), [("lq", 0)], ["sc0"])
        P.op("dve", lambda e: e.tensor_reduce(out=sc[:, 1:2], in_=lq[:, 2, :], axis=AX.X, op=ALU.add), [("lq", 2)], ["sc1"])
        act(P, sc[:, 2:3], sc[:, 0:1], AF.Exp, ["sc0"], ["sc2"])
        act(P, sc[:, 3:4], sc[:, 1:2], AF.Exp, ["sc1"], ["sc3"])
        tt(P, "dve", sc[:, 4:5], sc[:, 3:4], sc[:, 2:3], ALU.subtract, ["sc2", "sc3"], ["sc4"])
        ts(P, "dve", sc[:, 5:6], sc[:, 4:5], -lam_init, None, ALU.add, None, ["sc4"], ["neglam"])
        P.dma("sp", sg[:], subln[a].rearrange("(c p) -> p c", p=128), w=["sg"], allow_slow_non_contiguous=True)
        ts(P, "dve", sg[:], sg[:], 1.0 - lam_init, None, ALU.mult, None, ["sg"], ["sg"])
        P.flush()
        neglam = sc[:, 5:6]
        eps_ap = sc[:, 6:7]

        pend = []

        def epi2(h, tok0, nq, is_ctx):
            mm(P, Sps[0][:, 0:nq], ones_f[:], Sq[:, 0, 0:nq], True, False, [("sq", 0)], [("S", 0)])
            mm(P, Sps[0][:, 0:nq], ones_f[:], Sq[:, 1, 0:nq], False, True, [("sq", 1)], [("S", 0)])
            act(P, Rt[:, 0:nq], Sps[0][:, 0:nq], AF.Ln, [("S", 0)], ["rt"], scale=1.0 / 256.0, bias=eps_ap)
            act(P, Rt[:, 0:nq], Rt[:, 0:nq], AF.Exp, ["rt"], ["rt"], scale=-0.5)
            gsrc = B.t_gtc() if is_ctx else gt
            zdst = B.t_ztc() if is_ctx else zt
            for dv in range(2):
                P.dma("pool", Gs[:, dv, 0:nq], gsrc[h * 2 + dv, :, tok0:tok0 + nq], w=[("g", dv)])
                stt(P, Ys[:, dv, 0:nq], Oo[:, dv, 0:nq], sg[:, dv:dv + 1], Rt[:, 0:nq], ALU.mult, ALU.mult,
                    [("o", dv), "rt"], [("ab", dv)])
                tt(P, "pool", Zs[:, dv, 0:nq], Ys[:, dv, 0:nq], Gs[:, dv, 0:nq], ALU.mult, [("ab", dv), ("g", dv)], [("z", dv)])
                P.dma("sp", zdst[h * 2 + dv, :, tok0:tok0 + nq], Zs[:, dv, 0:nq], r=[("z", dv)])

        kt_own, v_own = B.t_kt_own(), B.t_v_own()
        HN = NOWN // 2
        hsems = []
        for h in range(H):
            _ccuid[0] += 1
            cm = nc.semaphore("cc%d" % _ccuid[0])
            hsems.append(cm.__enter__())
            P._cms.append(cm)
        for h in range(H):
            prs = [(kt_own[h * 2 + c], kt_all[h * 2 + c].rearrange("r p n -> (r p) n")) for c in range(2)]
            prs += [(v_own[h, hf * HN:(hf + 1) * HN, :], v_all[h, hf].rearrange("r t n -> (r t) n")) for hf in range(2)]
            for (ci, co) in prs:
                o_ = P.op("pool", (lambda e, ci=ci, co=co: e.collective_compute(
                    "AllGather", ALU.bypass, replica_groups=RG, ins=[ci.opt()], outs=[co.opt()])), (), ())
                o_.cinc = (hsems[h], 1)
        qn = 0
        for h in range(H):
            hw = [(hsems[h], 4)]
            for c in range(2):
                hc = h * 2 + c
                P.dma("sp", Kt[:, c, 0:CTX], ktc[hc], w=[("K", c, 0), ("K", c, 1)])
                for r in range(4):
                    for t8 in range(0, NTO, 8):
                        n8 = min(8, NTO - t8)
                        k0 = CTX + r * NOWN + t8 * 128
                        P.dma("sp" if (r + c) % 2 == 0 else "pool", Kt[:, c, k0:k0 + n8 * 128],
                              kt_all[hc, r, :, t8 * 128:(t8 + n8) * 128],
                              w=[("K", c, 2 + r * NTO + t8 + i) for i in range(n8)], xw=hw)
            P.dma("sp", Vt[:, 0:2, :], vc[:, h * 256:(h + 1) * 256].rearrange("(t p) n -> p t n", p=128),
                  w=[("V", 0), ("V", 1)])
            tph = HN // 128
            for r in range(4):
                for hf in range(2):
                    for t8 in range(0, tph, 8):
                        n8 = min(8, tph - t8)
                        kt0 = 2 + r * NTO + hf * tph + t8
                        P.dma("pool" if (r + hf) % 2 == 0 else "sp", Vt[:, kt0:kt0 + n8, :],
                              v_all[h, hf, r, t8 * 128:(t8 + n8) * 128, :].rearrange("(t p) n -> p t n", p=128),
                              w=[("V", kt0 + i) for i in range(n8)], xw=hw)
            chunks = [(False, q0, min(QW, NOWN - q0)) for q0 in range(0, NOWN, QW)]
            if ctx_q:
                chunks.append((True, 0, CTX))
            for (is_ctx, tok0, nq) in chunks:
                qs = qn % 2
                qn += 1
                qsrc = B.t_qtc() if is_ctx else qt
                for c in range(2):
                    P.dma("sp", Qs[:, qs, c, 0:nq], qsrc[h * 2 + c, :, tok0:tok0 + nq], w=[("Q", qs, c)])
                kts = list(range(2)) if is_ctx else list(range(NKT))
                nkt = len(kts)

                def emit_S(i):
                    kt = kts[i]
                    for c in range(2):
                        mm(P, Sps[c][:, 0:nq], Kt[:, c, kt * 128:(kt + 1) * 128], Qs[:, qs, c, 0:nq], True, True,
                           [("K", c, kt), ("Q", qs, c)], [("S", c)])
                        act(P, PT[:, i % 4, c, 0:nq], Sps[c][:, 0:nq], AF.Exp, [("S", c)], [("PT", i % 4, c)], scale=scale)

                def emit_AV(i):
                    kt = kts[i]
                    st, sp_ = (i == 0), (i == nkt - 1)
                    for dv in range(2):
                        for c in range(2):
                            mm(P, Ops[c][dv][:, 0:nq], Vt[:, kt, dv * 128:(dv + 1) * 128], PT[:, i % 4, c, 0:nq], st, sp_,
                               [("V", kt), ("PT", i % 4, c)], [("O", c, dv)])
                    g0 = (i // GL) * GL
                    gsz = min(GL, nkt - g0)
                    gi = i - g0
                    last = gi == gsz - 1
                    gpar = (i // GL) % 2
                    for c in range(2):
                        eng = "dve" if c == 0 else "pool"
                        if gsz == 1:
                            mm(P, Lps[c][:, 0:nq], ones_b[:], PT[:, i % 4, c, 0:nq], g0 == 0, sp_, [("PT", i % 4, c)], [("L", c)])
                            continue
                        if gi == 0:
                            continue
                        dst, dkey = (Paccb[:, gpar, c, 0:nq], ("paccb", gpar, c)) if last else (Pacc[:, c, 0:nq], ("pacc", c))
                        if gi == 1:
                            tt(P, eng, dst, PT[:, (i - 1) % 4, c, 0:nq], PT[:, i % 4, c, 0:nq], ALU.add,
                               [("PT", (i - 1) % 4, c), ("PT", i % 4, c)], [dkey])
                        else:
                            tt(P, eng, dst, Pacc[:, c, 0:nq], PT[:, i % 4, c, 0:nq], ALU.add,
                               [("pacc", c), ("PT", i % 4, c)], [dkey])
                        if last:
                            mm(P, Lps[c][:, 0:nq], ones_b[:], Paccb[:, gpar, c, 0:nq], g0 == 0, sp_, [("paccb", gpar, c)], [("L", c)])

                emit_S(0)
                for i in range(nkt):
                    if i + 1 < nkt:
                        emit_S(i + 1)
                    emit_AV(i)
                    if i == min(16, nkt - 1) and pend:
                        epi2(*pend.pop())
                if pend:
                    epi2(*pend.pop())
                for c in range(2):
                    cp(P, "dve", Lc[:, c, 0:nq], Lps[c][:, 0:nq], [("L", c)], [("lc", c)])
                    for dv in range(2):
                        if (c + dv) % 2 == 0:
                            cp(P, "dve", Oc[:, c * 2 + dv, 0:nq], Ops[c][dv][:, 0:nq], [("O", c, dv)], [("oc", c, dv)])
                        else:
                            act(P, Oc[:, c * 2 + dv, 0:nq], Ops[c][dv][:, 0:nq], AF.Copy, [("O", c, dv)], [("oc", c, dv)])
                for c in range(2):
                    recip(P, Lc[:, c, 0:nq], Lc[:, c, 0:nq], [("lc", c)], [("lc", c)])
                for dv in range(2):
                    tt(P, "pool", AB[:, 0, 0:nq], Oc[:, dv, 0:nq], Lc[:, 0, 0:nq], ALU.mult, [("oc", 0, dv), ("lc", 0)], [("ab", 0)])
                    tt(P, "pool", AB[:, 1, 0:nq], Oc[:, 2 + dv, 0:nq], Lc[:, 1, 0:nq], ALU.mult, [("oc", 1, dv), ("lc", 1)], [("ab", 1)])
                    stt(P, Oo[:, dv, 0:nq], AB[:, 1, 0:nq], neglam, AB[:, 0, 0:nq], ALU.mult, ALU.add,
                        [("ab", 0), ("ab", 1)], [("o", dv)])
                    tt(P, "pool", Sq[:, dv, 0:nq], Oo[:, dv, 0:nq], Oo[:, dv, 0:nq], ALU.mult, [("o", dv)], [("sq", dv)])
                pend.append((h, tok0, nq, is_ctx))
        if pend:
            epi2(*pend.pop())
        P.flush()


GL = 32
N_BAND = 8


def stage_poolmix(B, layer):
    nc, P, cfg = B.nc, B.P, B.cfg
    a = layer // 2
    ctx_out = layer == 1
    NOWN, NTO = cfg.NOWN, cfg.NTO
    U = B.t_u()
    gt, zt = B.t_gt(), B.t_zt()
    grp_w = B.dram("pool_grp_w", [2, 4, 512, 512], F32)[a]
    pscale = B.dram("pool_scale", [2, D], F32)[a]
    bands = B.dram("bands", [4, 9, 128, 128], F32)
    A = Alloc(nc)
    bd = A.sb("m_bd", [128, 4, 9, 128], BF16)
    gw = A.sb("m_gw", [128, 4, 4, 512], BF16)
    psc = A.sb("m_ps", [128, KC], F32)
    us = A.sb("m_u", [128, 6, D], BF16)
    pl = A.sb("m_pl", [128, KC, 512], BF16)
    gs = A.sb("m_g", [128, KC, 512], BF16)
    ys = A.sb("m_y", [128, 2, 512], F32)
    zs = A.sb("m_z", [128, KC, 512], BF16)
    p0 = A.ps("m_p0", [128, 512], F32)
    p1 = A.ps("m_p1", [128, 512], F32)
    p2 = A.ps("m_p2", [128, 512], F32)
    p3 = A.ps("m_p3", [128, 512], F32)
    with A:
        pp = [p0, p1, p2, p3]
        P.dma("pool", bd[:], bands.rearrange("g v p n -> p g v n"), w=["bd"])
        for g in range(4):
            P.dma("pool", gw[:, g, :, :], grp_w[g].rearrange("(c p) e -> p c e", p=128), w=[("gw", g)])
        load_featmajor_vec(P, "sp", psc[:], pscale, ["psc"])
        P.flush()
        chunks = [("own", t0, min(4, NTO - t0)) for t0 in range(0, NTO, 4)]
        if ctx_out:
            chunks.append(("ctx", 0, 2))
        cnt = {"p": 0, "y": 0}
        for (kind, t0, nt) in chunks:
            nq = nt * 128
            if kind == "own":
                prev_row = (t0 - 1) * 128 if t0 > 0 else NOWN
                next_row = (t0 + nt) * 128 if t0 + nt < NTO else NOWN
                P.dma("sp", us[:, 0, :], U[prev_row:prev_row + 128, :], w=[("u", 0)])
                P.dma("pool", us[:, 1:1 + nt, :], U[t0 * 128:(t0 + nt) * 128, :].rearrange("(t p) n -> p t n", p=128),
                      w=[("u", 1 + i) for i in range(nt)])
                P.dma("sp", us[:, 1 + nt, :], U[next_row:next_row + 128, :], w=[("u", 1 + nt)])
                gsrc, zdst, tok0 = gt, zt, t0 * 128
            else:
                base = NOWN + 128
                P.dma("pool", us[:, 1:3, :], U[base:base + 256, :].rearrange("(t p) n -> p t n", p=128), w=[("u", 1), ("u", 2)])
                gsrc, zdst, tok0 = B.t_gtc(), B.t_ztc(), 0
            P.dma("sp", gs[:, :, 0:nq], gsrc[:, :, tok0:tok0 + nq].rearrange("c p n -> p c n"), w=["gs"])
            for ch in range(KC):
                g = ch // 4
                pi = cnt["p"] % 4
                cnt["p"] += 1
                for j in range(nt):
                    if kind == "own":
                        tglob = t0 + j
                        terms = []
                        terms.append((j, 0 if tglob == 0 else 1))
                        terms.append((j + 1, 2 if tglob == 0 else (4 if tglob == NTO - 1 else 3)))
                        terms.append((j + 2, 6 if tglob == NTO - 1 else 5))
                    else:
                        terms = [(1, 7), (2, 5)] if j == 0 else [(1, 1), (2, 8)]
                    for n, (slot, var) in enumerate(terms):
                        mm(P, pp[pi][:, j * 128:(j + 1) * 128], us[:, slot, ch * 128:(ch + 1) * 128], bd[:, g, var, :],
                           n == 0, n == len(terms) - 1, [("u", slot)], [("p", pi)])
                if ch % 2 == 0:
                    act(P, pl[:, ch, 0:nq], pp[pi][:, 0:nq], AF.Copy, [("p", pi)], [("pl", ch)])
                else:
                    cp(P, "dve", pl[:, ch, 0:nq], pp[pi][:, 0:nq], [("p", pi)], [("pl", ch)])
            for g in range(4):
                for ec in range(4):
                    pi = cnt["p"] % 4
                    cnt["p"] += 1
                    for cc in range(4):
                        mm(P, pp[pi][:, 0:nq], gw[:, g, cc, ec * 128:(ec + 1) * 128], pl[:, g * 4 + cc, 0:nq], cc == 0, cc == 3,
                           [("pl", g * 4 + cc)], [("p", pi)])
                    e = g * 4 + ec
                    stt(P, zs[:, e, 0:nq], pp[pi][:, 0:nq], psc[:, e:e + 1], gs[:, e, 0:nq], ALU.mult, ALU.mult,
                        [("p", pi), "gs"], [("z", e)])
            P.dma("sp", zdst[:, :, tok0:tok0 + nq].rearrange("c p n -> p c n"), zs[:, :, 0:nq], r=[("z", e) for e in range(KC)])
        P.flush()


def stage_out(B, layer):
    nc, P, cfg = B.nc, B.P, B.cfg
    is_attn = layer % 2 == 0
    a = layer // 2
    ctx_out = layer in (0, 1)
    NOWN, NTO = cfg.NOWN, cfg.NTO
    w_out = (B.dram("attn_w_out", [2, D, D], F32) if is_attn else B.dram("pool_w_out", [2, D, D], F32))[a]
    modv = B.t_modv()
    ln_g = B.dram("ln_g", [DEPTH, D], F32)
    ln_b = B.dram("ln_b", [DEPTH, D], F32)
    x_src, x_dst = B.t_x(layer), B.t_x(layer + 1)
    zt = B.t_zt()
    groups = [("own", t) for t in range(NTO)]
    if ctx_out:
        groups += [("ctx", 0), ("ctx", 1)]
    A = Alloc(nc)
    wo = A.sb("o_w", [128, KC, D], BF16)
    gate = A.sb("o_gate", [128, 2, D], F32)
    lng = A.sb("o_lng", [128, D], F32)
    lnb = A.sb("o_lnb", [128, D], F32)
    z0 = A.sb("o_z0", [128, KC, 128], BF16)
    z1 = A.sb("o_z1", [128, KC, 128], BF16)
    x0 = A.sb("o_x0", [128, D], F32)
    x1 = A.sb("o_x1", [128, D], F32)
    r0 = A.sb("o_r0", [128, D], F32)
    r1 = A.sb("o_r1", [128, D], F32)
    stats = A.sb("o_st", [128, 2, 24], F32)
    mv = A.sb("o_mv", [128, 2, 4], F32)
    pA = A.ps("o_p0", [128, D], F32)
    pB = A.ps("o_p1", [128, D], F32)
    with A:
        zz, xx, rr, pps = [z0, z1], [x0, x1], [r0, r1], [pA, pB]
        for kq in range(4):
            P.dma("pool", wo[:, kq * 4:(kq + 1) * 4, :], w_out[kq * 512:(kq + 1) * 512, :].rearrange("(k p) n -> p k n", p=128), w=[("wo", kq)])
        for who in range(2):
            P.dma("sp", gate[:, who, :], modv[layer, who:who + 1, 2 * D:3 * D].partition_broadcast(128), w=[("gate", who)])
        P.dma("sp", lng[:], ln_g[layer:layer + 1, :].partition_broadcast(128), w=["lng"])
        P.dma("sp", lnb[:], ln_b[layer:layer + 1, :].partition_broadcast(128), w=["lnb"])
        P.op("dve", lambda e: e.memset(mv[:, :, 3:4], LN_EPS), (), ["eps"])
        P.flush()
        for n, (kind, t) in enumerate(groups):
            s = n % 2
            who = 1 if kind == "ctx" else 0
            if kind == "own":
                zsrc = zt[:, :, t * 128:(t + 1) * 128]
                xs_ap = x_src[t * 128:(t + 1) * 128, :]
                xd_ap = x_dst[t * 128:(t + 1) * 128, :]
            else:
                zsrc = B.t_ztc()[:, :, t * 128:(t + 1) * 128]
                xs_ap = B.t_c(layer)[t * 128:(t + 1) * 128, :]
                xd_ap = B.t_c(layer + 1)[t * 128:(t + 1) * 128, :]
            P.dma("sp", zz[s][:], zsrc.rearrange("c p n -> p c n"), w=[("z", s)])
            P.dma("pool", xx[s][:], xs_ap, w=[("x", s)])
            for nb in range(4):
                for k in range(KC):
                    mm(P, pps[s][:, nb * 512:(nb + 1) * 512], zz[s][:, k, :], wo[:, k, nb * 512:(nb + 1) * 512], k == 0, k == KC - 1,
                       [("z", s)], [("p", s, nb)])
            for nb in range(4):
                sl = slice(nb * 512, (nb + 1) * 512)
                tt(P, "dve", rr[s][:, sl], pps[s][:, sl], gate[:, who, sl], ALU.mult, [("p", s, nb)], [("r", s, nb)])
                stt(P, rr[s][:, sl], xx[s][:, sl], ALU_ALPHA, rr[s][:, sl], ALU.mult, ALU.add, [("x", s), ("r", s, nb)], [("r", s, nb)])
                P.op("dve", (lambda e, s=s, nb=nb, sl=sl: e.bn_stats(out=stats[:, s, nb * 6:(nb + 1) * 6], in_=rr[s][:, sl])),
                     [("r", s, nb)], [("st", s, nb)])
            P.op("dve", (lambda e, s=s: e.bn_aggr(out=mv[:, s, 0:2], in_=stats[:, s, :])),
                 [("st", s, nb) for nb in range(4)], [("mv", s)])
            act(P, mv[:, s, 2:3], mv[:, s, 1:2], AF.Sqrt, [("mv", s)], [("rstd", s)], bias=mv[:, s, 3:4])
            recip(P, mv[:, s, 2:3], mv[:, s, 2:3], [("rstd", s)], [("rstd", s)])
            rkeys = [("r", s, nb) for nb in range(4)]
            ts(P, "dve", rr[s][:], rr[s][:], mv[:, s, 0:1], mv[:, s, 2:3], ALU.subtract, ALU.mult, rkeys + [("mv", s), ("rstd", s)], rkeys)
            tt(P, "pool", rr[s][:], rr[s][:], lng[:], ALU.mult, rkeys, rkeys)
            tt(P, "pool", rr[s][:], rr[s][:], lnb[:], ALU.add, rkeys, rkeys)
            P.dma("sp", xd_ap, rr[s][:], r=rkeys)
        P.flush()


ALU_ALPHA = float(ALPHA)

RG = [[0, 1, 2, 3], [4, 5, 6, 7]]
_ccuid = [0]


def collective_allgather(B, pairs):
    nc = B.nc
    B.P.flush()
    _ccuid[0] += 1
    cm = nc.semaphore("cc%d" % _ccuid[0])
    sm = cm.__enter__()
    B.P._cms.append(cm)
    with nc.Block() as block:
        def body(g):
            for (i, o) in pairs:
                g.collective_compute("AllGather", ALU.bypass, replica_groups=RG, ins=[i.opt()], outs=[o.opt()]).then_inc(sm, 1)
            g.wait_ge(sm, len(pairs))
            g.nop()
        block.gpsimd(body)


def stage_exch_kv(B):
    c = B.cfg
    kt_own, kt_all, v_own, v_all = B.t_kt_own(), B.t_kt_all(), B.t_v_own(), B.t_v_all()
    pairs = []
    for hc in range(16):
        pairs.append((kt_own[hc], kt_all[hc].rearrange("r p n -> (r p) n")))
    for vcn in range(c.NVC):
        pairs.append((v_own[vcn * c.VCH:(vcn + 1) * c.VCH, :], v_all[vcn].rearrange("r t n -> (r t) n")))
    collective_allgather(B, pairs)


def stage_exch_halo(B, layer):
    nc, P, cfg = B.nc, B.P, B.cfg
    NOWN = cfg.NOWN
    x_src = B.t_x(layer)
    xedge = B.dram("xedge", [16, D], F32)
    xedge_all = B.dram("xedge_all", [64, D], F32)
    sel = B.dram("halo_sel", [64, 128], F32)
    xhalo = B.t_xhalo()
    P.dma("sp", xedge[0:8, :], x_src[0:8, :])
    P.dma("sp", xedge[8:16, :], x_src[NOWN - 8:NOWN, :])
    collective_allgather(B, [(xedge, xedge_all)])
    A = Alloc(nc)
    E = A.sb("h_e", [64, D], F32)
    S = A.sb("h_s", [64, 128], F32)
    Hs = A.sb("h_h", [128, D], F32)
    ps = A.ps("h_ps", [128, D], F32)
    with A:
        P.dma("sp", E[:], xedge_all, w=["E"])
        P.dma("sp", S[:], sel, w=["S"])
        for nb in range(4):
            sl = slice(nb * 512, (nb + 1) * 512)
            mm(P, ps[:, sl], S[:], E[:, sl], True, True, ["E", "S"], [("ps", nb)])
            act(P, Hs[:, sl], ps[:, sl], AF.Copy, [("ps", nb)], [("h", nb)])
        P.dma("sp", xhalo, Hs[:], r=[("h", nb) for nb in range(4)])
        P.flush()


def _band(in_off, out_off, S_seq, w):
    lo = w // 2
    hi = w - 1 - lo
    t_out = out_off + np.arange(128)
    t_in = in_off + np.arange(128)
    cnt = (np.minimum(t_out + hi + 1, S_seq) - np.maximum(t_out - lo, 0)).astype(np.float64)
    cnt = np.maximum(cnt, 1.0)
    inwin = (t_in[:, None] >= t_out[None, :] - lo) & (t_in[:, None] <= t_out[None, :] + hi)
    inwin &= (t_in[:, None] >= 0) & (t_in[:, None] < S_seq)
    m = inwin / cnt[None, :] - (t_in[:, None] == t_out[None, :])
    valid_out = (t_out >= 0) & (t_out < S_seq)
    m = m * valid_out[None, :]
    return m.astype(np.float32)


def _consts(cfg, j):
    S, NOWN = cfg.S, cfg.NOWN
    own0, own1 = j * NOWN, (j + 1) * NOWN
    BIG = 1 << 20
    mid = 1 << 10
    bands = np.zeros((4, 9, 128, 128), np.float32)
    for g, w in enumerate(POOL_WINDOWS):
        bands[g, 0] = _band(own0 - 128, own0, S, w)
        bands[g, 1] = _band(mid * 128 - 128, mid * 128, BIG, w)
        bands[g, 2] = _band(own0, own0, S, w)
        bands[g, 3] = _band(mid * 128, mid * 128, BIG, w)
        bands[g, 4] = _band(own1 - 128, own1 - 128, S, w)
        bands[g, 5] = _band(mid * 128 + 128, mid * 128, BIG, w)
        bands[g, 6] = _band(own1, own1 - 128, S, w)
        bands[g, 7] = _band(0, 0, CTX, w)
        bands[g, 8] = _band(128, 128, CTX, w)
    t = (own0 + np.arange(NOWN)).astype(np.float32)
    t_row = np.floor(t / 64.0).astype(np.float32)
    t_col = (t - t_row * 64.0).astype(np.float32)
    inv_freq = (np.float32(10000.0) ** (-np.arange(32, dtype=np.float32) / np.float32(32))).astype(np.float32)
    ang_r = (t_row[None, :] * inv_freq[:, None]).astype(np.float32)
    ang_c = (t_col[None, :] * inv_freq[:, None]).astype(np.float32)
    cr, sr, cc_, sc_ = np.cos(ang_r), np.sin(ang_r), np.cos(ang_c), np.sin(ang_c)
    rope_c = np.concatenate([cr, cr, cc_, cc_], axis=0).astype(np.float32)
    rope_s = np.concatenate([sr, -sr, sc_, -sc_], axis=0).astype(np.float32)
    sel = np.zeros((64, 128), np.float32)
    for i in range(8):
        if j > 0:
            sel[(j - 1) * 16 + 8 + i, 120 + i] = 1.0
        if j < 3:
            sel[(j + 1) * 16 + i, i] = 1.0
    return {"bands": bands, "rope_c": np.ascontiguousarray(rope_c), "rope_s": np.ascontiguousarray(rope_s),
            "ident": np.eye(128, dtype=np.float32), "halo_sel": sel}


STAGES = {
    "mod": (stage_mod, None), "proj": stage_proj, "attn": stage_attn, "poolmix": stage_poolmix, "out": stage_out,
}

ALL_NAMES = ["x0", "x1", "x2", "x3", "c0", "c1", "c2", "modv", "QT", "GT", "QTc", "GTc", "KTc", "Vc", "KT_own", "V_own",
             "KT_all", "V_all", "U", "ZT", "ZTc", "xhalo", "cvec", "mod_w", "mod_b", "ln_g", "ln_b", "attn_w_in", "attn_w_out",
             "attn_lq1", "attn_lk1", "attn_lq2", "attn_lk2", "attn_subln_g", "pool_w_in", "pool_grp_w", "pool_scale",
             "pool_w_out", "ident", "rope_c", "rope_s", "bands", "halo_sel"]


def _produced(stage):
    name, layer = stage
    if name == "mod":
        return {"modv"}
    if name == "proj":
        return {"QT", "GT", "QTc", "GTc", "KTc", "Vc", "KT_own", "V_own"} if layer % 2 == 0 else {"U", "GT", "GTc"}
    if name == "attn":
        return {"ZT", "ZTc", "KT_all", "V_all"}
    if name == "poolmix":
        return {"ZT", "ZTc"}
    if name == "out":
        return {"x%d" % (layer + 1), "c%d" % (layer + 1)}
    if name == "exch_kv":
        return {"KT_all", "V_all"}
    if name == "exch_halo":
        return {"xhalo"}
    return set()


def _uses(B, stage):
    name, layer = stage
    c = B.cfg
    if name == "mod":
        B.dram("cvec", [2, D], F32); B.dram("mod_w", [DEPTH, D, 3 * D // 4], F32); B.dram("mod_b", [DEPTH, 3 * D // 4], F32); B.t_modv()
        B.dram("modp", [DEPTH * 2, 3 * D // 4], F32); B.dram("modg", [4 * DEPTH * 2, 3 * D // 4], F32)
    elif name == "proj":
        if layer % 2 == 0:
            B.dram("attn_w_in", [2, D, 4 * D], F32)
        else:
            B.dram("pool_w_in", [2, D, 2 * D], F32)
        B.t_modv(); B.dram("ident", [128, 128], F32); B.dram("rope_c", [128, c.NOWN], F32); B.dram("rope_s", [128, c.NOWN], F32)
        B.t_x(layer)
        if layer <= 2:
            B.t_c(layer)
        if layer % 2 == 0:
            B.t_qt(); B.t_gt(); B.t_qtc(); B.t_gtc(); B.t_ktc(); B.t_vc(); B.t_kt_own(); B.t_v_own()
        else:
            B.t_xhalo(); B.t_u(); B.t_gt(); B.t_gtc()
    elif name == "attn":
        B.t_qt(); B.t_gt(); B.t_ktc(); B.t_vc(); B.t_kt_all(); B.t_v_all(); B.t_zt(); B.t_kt_own(); B.t_v_own()
        for n in ("attn_lq1", "attn_lk1", "attn_lq2", "attn_lk2"):
            B.dram(n, [2, 128], F32)
        B.dram("attn_subln_g", [2, 256], F32)
        if layer == 0:
            B.t_qtc(); B.t_gtc(); B.t_ztc()
    elif name == "poolmix":
        B.t_u(); B.t_gt(); B.t_zt(); B.dram("pool_grp_w", [2, 4, 512, 512], F32); B.dram("pool_scale", [2, D], F32)
        B.dram("bands", [4, 9, 128, 128], F32)
        if layer == 1:
            B.t_gtc(); B.t_ztc()
    elif name == "exch_kv":
        B.t_kt_own(); B.t_kt_all(); B.t_v_own(); B.t_v_all()
    elif name == "exch_halo":
        B.t_x(layer); B.dram("xedge", [16, D], F32); B.dram("xedge_all", [64, D], F32); B.dram("halo_sel", [64, 128], F32); B.t_xhalo()
    elif name == "out":
        B.dram("attn_w_out" if layer % 2 == 0 else "pool_w_out", [2, D, D], F32)
        B.t_modv(); B.dram("ln_g", [DEPTH, D], F32); B.dram("ln_b", [DEPTH, D], F32)
        B.t_x(layer); B.t_x(layer + 1); B.t_zt()
        if layer in (0, 1):
            B.t_ztc(); B.t_c(layer); B.t_c(layer + 1)


def build_launch(cfg, stages, outs):
    produced = set()
    for st in stages:
        produced |= _produced(st)
    ext_in = [n for n in ALL_NAMES if n not in produced]
    B = Build(cfg, ext_in, outs)
    for st in stages:
        _uses(B, st)
    for (name, layer) in stages:
        if name == "mod":
            stage_mod(B)
        elif name == "proj":
            stage_proj(B, layer)
        elif name == "attn":
            stage_attn(B, layer)
        elif name == "poolmix":
            stage_poolmix(B, layer)
        elif name == "out":
            stage_out(B, layer)
        elif name == "exch_kv":
            stage_exch_kv(B)
        elif name == "exch_halo":
            stage_exch_halo(B, layer)
        else:
            raise ValueError(name)
    done = B.nc.dram_tensor("done", [1, 16], F32, kind="ExternalOutput").ap()
    B.used_out.append("done")
    A = Alloc(B.nc)
    dn = A.sb("dn", [1, 16], F32)
    with A:
        B.P.op("dve", lambda e: e.memset(dn[:], 1.0), (), ["dn"])
        B.P.dma("sp", done, dn[:], r=["dn"])
        B.P.flush()
    B.P.close()
    return B


def run_launch(cfg, stages, outs, pool):
    B = build_launch(cfg, stages, outs)
    def _in(a):
        a = np.asarray(a)
        return a.view(np.uint16) if a.dtype == ml_dtypes.bfloat16 else a
    in_maps = [{n: _in(pool[r][n]) for n in B.used_in} for r in range(8)]
    res = run_bass_kernel_spmd(B.nc, in_maps, core_ids=list(range(8)))
    for r in range(8):
        for n in B.used_out:
            a = np.asarray(res.results[r][n])
            pool[r][n] = a.view(ml_dtypes.bfloat16) if a.dtype == np.uint16 else a
    return B


def _host_exchange_kv(pool):
    for b in range(2):
        kt = np.ascontiguousarray(np.stack([pool[b * 4 + r]["KT_own"] for r in range(4)], axis=1))
        v = np.stack([pool[b * 4 + r]["V_own"] for r in range(4)], axis=0)
        nown = v.shape[1]
        vch = min(256, nown)
        v = np.ascontiguousarray(v.reshape(4, nown // vch, vch, D).transpose(1, 0, 2, 3))
        for r in range(4):
            pool[b * 4 + r]["KT_all"] = kt
            pool[b * 4 + r]["V_all"] = v


def _host_exchange_halo(pool, name):
    for b in range(2):
        for j in range(4):
            hal = np.zeros((128, D), np.float32)
            if j > 0:
                hal[120:128] = pool[b * 4 + j - 1][name][-8:]
            if j < 3:
                hal[0:8] = pool[b * 4 + j + 1][name][:8]
            pool[b * 4 + j]["xhalo"] = hal


def make_pool(cfg, inputs):
    f = lambda a: np.ascontiguousarray(np.asarray(a, dtype=np.float32))
    x, c, ctx, c_ctx = f(inputs["x"]), f(inputs["c"]), f(inputs["ctx"]), f(inputs["c_ctx"])
    mod_w_full, mod_b_full = f(inputs["mod_w"]), f(inputs["mod_b"])
    MC = 3 * D // 4
    shared = {k: f(inputs[k]) for k in ("ln_g", "ln_b", "attn_w_in", "attn_w_out", "attn_lq1", "attn_lk1",
                                         "attn_lq2", "attn_lk2", "attn_subln_g", "pool_w_in", "pool_grp_w", "pool_scale",
                                         "pool_w_out")}
    pool = []
    for r in range(8):
        b, j = r // 4, r % 4
        d = dict(shared)
        d["x0"] = np.ascontiguousarray(x[b, j * cfg.NOWN:(j + 1) * cfg.NOWN])
        d["c0"] = np.ascontiguousarray(ctx[b])
        d["cvec"] = np.ascontiguousarray(np.stack([c[b], c_ctx], axis=0))
        d["mod_w"] = np.ascontiguousarray(mod_w_full[:, :, j * MC:(j + 1) * MC])
        d["mod_b"] = np.ascontiguousarray(mod_b_full[:, j * MC:(j + 1) * MC])
        d.update(_consts(cfg, j))
        pool.append(d)
    return pool


def run_multi(inputs, nl=DEPTH):
    S = int(np.asarray(inputs["x"]).shape[1])
    cfg = Cfg(S)
    pool = make_pool(cfg, inputs)
    kvo = ["modv", "QT", "GT", "QTc", "GTc", "KTc", "Vc", "KT_own", "V_own"]
    run_launch(cfg, [("mod", 0), ("proj", 0)], kvo, pool)
    _host_exchange_kv(pool)
    run_launch(cfg, [("attn", 0), ("out", 0)], ["x1", "c1"], pool)
    if nl >= 2:
        _host_exchange_halo(pool, "x1")
        st = [("proj", 1), ("poolmix", 1), ("out", 1)]
        outs = ["x2", "c2"]
        if nl >= 3:
            st.append(("proj", 2))
            outs += kvo[1:]
        run_launch(cfg, st, outs, pool)
    if nl >= 3:
        _host_exchange_kv(pool)
        run_launch(cfg, [("attn", 2), ("out", 2)], ["x3"], pool)
    if nl >= 4:
        _host_exchange_halo(pool, "x3")
        run_launch(cfg, [("proj", 3), ("poolmix", 3), ("out", 3)], ["x4"], pool)
    name = "x%d" % nl
    out = np.stack([np.concatenate([pool[b * 4 + j][name] for j in range(4)], axis=0) for b in range(2)], axis=0)
    return out.astype(np.float32)


FUSED_STAGES = [("mod", 0), ("proj", 0), ("attn", 0), ("out", 0),
                ("exch_halo", 1), ("proj", 1), ("poolmix", 1), ("out", 1),
                ("proj", 2), ("attn", 2), ("out", 2),
                ("exch_halo", 3), ("proj", 3), ("poolmix", 3), ("out", 3)]


def run_fused(inputs):
    S = int(np.asarray(inputs["x"]).shape[1])
    cfg = Cfg(S)
    pool = make_pool(cfg, inputs)
    run_launch(cfg, FUSED_STAGES, ["x4"], pool)
    out = np.stack([np.concatenate([pool[b * 4 + j]["x4"] for j in range(4)], axis=0) for b in range(2)], axis=0)
    return out.astype(np.float32)


def kernel(**inputs):
    return run_fused(inputs)
```
